# Optimizing a Trainium2 kernel written in Bass

```python
import math
import jax, jax.numpy as jnp
from jax import lax
import numpy as np

D_MODEL = 1024
BATCH = 16
SEQ = 4096
DEPTH = 4

HEAD_DIM = 64
N_HEADS_DIL = 8
DIL_WIDTH = N_HEADS_DIL * HEAD_DIM
N_HEADS_DIFF = 4
DIFF_WIDTH = N_HEADS_DIFF * 2 * HEAD_DIM
MIX_WIDTH = DIL_WIDTH + DIFF_WIDTH
IN_WIDTH = 3 * MIX_WIDTH
DILATIONS = ((128, 1), (512, 4), (2048, 16))
BAND = 128
N_BUCKETS = 32
MAX_DISTANCE = 2048
N_BIAS_HEADS = N_HEADS_DIL + N_HEADS_DIFF
D_FF = -(-8 * D_MODEL // (3 * 256)) * 256
Q_BLOCK = 128
EPS = 1e-6
SUBLN_EPS = 1e-5
NEG = -1e30

kernel_name = "hybrid_dilated_diffattn_block"


def rmsnorm(x, g, eps=EPS):
    xf = x.astype(jnp.float32)
    y = xf * lax.rsqrt(jnp.mean(xf * xf, axis=-1, keepdims=True) + eps)
    return (y * g.astype(jnp.float32)).astype(x.dtype)


def rel_bucket(dist):
    max_exact = N_BUCKETS // 2
    n = jnp.maximum(dist, 0)
    nf = jnp.maximum(n, 1).astype(jnp.float32)
    large = max_exact + (jnp.log(nf / max_exact) / math.log(MAX_DISTANCE / max_exact)
                         * (N_BUCKETS - max_exact)).astype(jnp.int32)
    large = jnp.minimum(large, N_BUCKETS - 1)
    return jnp.where(n < max_exact, n, large)


def dilated_branch(q, k, v, bias_tab, window, dil):
    B, S, H, hd = q.shape
    L = S // dil
    nb = -(-L // BAND)
    Lp = nb * BAND
    n_keys = window // dil

    def strided(t):
        t = t.reshape(B, L, dil, H, hd).transpose(0, 2, 1, 3, 4)
        t = jnp.pad(t, ((0, 0), (0, 0), (0, Lp - L), (0, 0), (0, 0)))
        return t.reshape(B, dil, nb, BAND, H, hd)

    def band(t):
        prev = jnp.pad(t, ((0, 0), (0, 0), (1, 0), (0, 0), (0, 0), (0, 0)))[:, :, :-1]
        return jnp.concatenate([prev, t], axis=3)

    qb = strided(q)
    kk = band(strided(k))
    vv = band(strided(v))
    scale = HEAD_DIM ** -0.5
    s = jnp.einsum('brnqhd,brnkhd->brnhqk', qb, kk,
                   preferred_element_type=jnp.float32) * scale
    i = jnp.arange(BAND)[:, None]
    j = jnp.arange(2 * BAND)[None, :]
    m = i + BAND - j
    bias = bias_tab[rel_bucket(m * dil)].transpose(2, 0, 1).astype(jnp.float32)
    valid = (m >= 0) & (m <= n_keys)
    blk = jnp.arange(nb)[:, None, None]
    mask = valid[None] & ((blk > 0) | (j >= BAND)[None])
    s = jnp.where(mask[None, None, :, None], s + bias, NEG)
    lse = jax.nn.logsumexp(s, axis=-1)
    p = jnp.exp(s - lse[..., None])
    o = jnp.einsum('brnhqk,brnkhd->brnqhd', p.astype(v.dtype), vv)
    o = o.reshape(B, dil, Lp, H, hd)[:, :, :L].transpose(0, 2, 1, 3, 4).reshape(B, S, H, hd)
    lse = lse.transpose(0, 1, 2, 4, 3).reshape(B, dil, Lp, H)[:, :, :L]
    lse = lse.transpose(0, 2, 1, 3).reshape(B, S, H)
    return o, lse


def dilated_attention(q, k, v, bias_tab):
    outs, lses = [], []
    for window, dil in DILATIONS:
        o, l = dilated_branch(q, k, v, bias_tab, window, dil)
        outs.append(o)
        lses.append(l)
    w = jax.nn.softmax(jnp.stack(lses, axis=0), axis=0)
    o = jnp.sum(w[..., None] * jnp.stack(outs, axis=0).astype(jnp.float32), axis=0)
    return o.astype(q.dtype)


def diff_attention(q, k, v, bias_tab, lam, subln_g, lam_init):
    B, S, H, _, hd = q.shape
    nqb = S // Q_BLOCK
    scale = HEAD_DIM ** -0.5
    qblocks = q.reshape(B, nqb, Q_BLOCK, H, 2, hd).transpose(1, 0, 2, 3, 4, 5)
    kpos = jnp.arange(S)

    def block(args):
        qb, b_idx = args
        qpos = b_idx * Q_BLOCK + jnp.arange(Q_BLOCK)
        dist = qpos[:, None] - kpos[None, :]
        bias = bias_tab[rel_bucket(dist)].transpose(2, 0, 1).astype(jnp.float32)
        s = jnp.einsum('bqhcd,bkhcd->bhcqk', qb, k,
                       preferred_element_type=jnp.float32) * scale
        s = jnp.where(dist >= 0, s + bias[None, :, None], NEG)
        p = jax.nn.softmax(s, axis=-1)
        a = p[:, :, 0] - lam * p[:, :, 1]
        return jnp.einsum('bhqk,bkhd->bqhd', a.astype(v.dtype), v)

    out = lax.map(block, (qblocks, jnp.arange(nqb)))
    out = out.transpose(1, 0, 2, 3, 4).reshape(B, S, H, 2 * hd)
    return rmsnorm(out, subln_g, SUBLN_EPS) * (1.0 - lam_init)


def setup_inputs(seed: int = 0) -> dict:
    key = jax.random.key(seed)
    ks = jax.random.split(key, 12)
    f32 = jnp.float32
    x = jax.random.normal(ks[0], (BATCH, SEQ, D_MODEL), f32)
    g_attn = 1.0 + 0.01 * jax.random.normal(ks[1], (DEPTH, D_MODEL), f32)
    w_in = jax.random.normal(ks[2], (DEPTH, D_MODEL, IN_WIDTH), f32) * D_MODEL ** -0.5
    w_out = jax.random.normal(ks[3], (DEPTH, MIX_WIDTH, D_MODEL), f32) * MIX_WIDTH ** -0.5
    rel_bias = 0.1 * jax.random.normal(ks[4], (N_BUCKETS, N_BIAS_HEADS), f32)
    lambda_qk = 0.1 * jax.random.normal(ks[5], (DEPTH, 4, HEAD_DIM), f32)
    subln_g = 1.0 + 0.01 * jax.random.normal(ks[6], (DEPTH, 2 * HEAD_DIM), f32)
    g_ffn = 1.0 + 0.01 * jax.random.normal(ks[7], (DEPTH, D_MODEL), f32)
    w_gate_up = jax.random.normal(ks[8], (DEPTH, D_MODEL, 2 * D_FF), f32) * D_MODEL ** -0.5
    w_down = jax.random.normal(ks[9], (DEPTH, D_FF, D_MODEL), f32) * D_FF ** -0.5
    g_final = 1.0 + 0.01 * jax.random.normal(ks[10], (D_MODEL,), f32)
    return {"x": x, "g_attn": g_attn, "w_in": w_in, "w_out": w_out,
            "rel_bias": rel_bias, "lambda_qk": lambda_qk, "subln_g": subln_g,
            "g_ffn": g_ffn, "w_gate_up": w_gate_up, "w_down": w_down,
            "g_final": g_final}


def reference(x, g_attn, w_in, w_out, rel_bias, lambda_qk, subln_g, g_ffn,
              w_gate_up, w_down, g_final):
    B, S, _ = x.shape
    bias_dil = rel_bias[:, :N_HEADS_DIL]
    bias_diff = rel_bias[:, N_HEADS_DIL:]
    for l in range(DEPTH):
        h = rmsnorm(x, g_attn[l])
        proj = h @ w_in[l]
        qa, ka, va, qb, kb, vb = jnp.split(proj, 6, axis=-1)
        hs = (B, S, N_HEADS_DIL, HEAD_DIM)
        oa = dilated_attention(qa.reshape(hs), ka.reshape(hs), va.reshape(hs), bias_dil)
        lam_init = 0.8 - 0.6 * math.exp(-0.3 * l)
        lq = lambda_qk[l].astype(jnp.float32)
        lam = (jnp.exp(jnp.sum(lq[0] * lq[1])) - jnp.exp(jnp.sum(lq[2] * lq[3]))
               + lam_init)
        ds = (B, S, N_HEADS_DIFF, 2, HEAD_DIM)
        ob = diff_attention(qb.reshape(ds), kb.reshape(ds),
                            vb.reshape(B, S, N_HEADS_DIFF, 2 * HEAD_DIM),
                            bias_diff, lam, subln_g[l], lam_init)
        mix = jnp.concatenate([oa.reshape(B, S, DIL_WIDTH),
                               ob.reshape(B, S, DIFF_WIDTH).astype(oa.dtype)], axis=-1)
        x = x + mix @ w_out[l]
        h = rmsnorm(x, g_ffn[l])
        gate, up = jnp.split(h @ w_gate_up[l], 2, axis=-1)
        x = x + (jax.nn.silu(gate) * up) @ w_down[l]
    return rmsnorm(x, g_final)
```

```python
import contextlib
import math

import numpy as np
import concourse.bass as bass
import concourse.mybir as mybir
from concourse.bass_utils import run_bass_kernel_spmd

F32 = mybir.dt.float32
BF16 = mybir.dt.bfloat16
ALU = mybir.AluOpType
AF = mybir.ActivationFunctionType
AX = mybir.AxisListType

D = 1024
S = 4096
DEPTH = 4
DFF = 2816
NFC = DFF // 128
NBLK = S // 128
EPS = 1e-6
SUBLN_EPS = 1e-5
L_DIL = 2560
L_DIFF = 4096
LA_DIL = L_DIL + 128
LA_DIFF = L_DIFF + 128


class _Op:
    __slots__ = ("eng", "fn", "deps", "sem", "signal", "ticket", "is_dma")


class Prog:
    ENG = ("pe", "act", "dve", "pool", "sp")

    def __init__(self, nc):
        self.nc = nc
        self.ops = []
        self.state = {}

    def add(self, eng, fn, reads=(), writes=(), partial=False, dsem=None):
        idx = len(self.ops)
        op = _Op()
        op.eng = eng
        op.fn = fn
        op.is_dma = dsem is not None
        op.sem = ("d", dsem) if op.is_dma else ("e", eng)
        op.signal = op.is_dma
        op.ticket = None
        deps = {}
        ops = self.ops

        def need(o):
            so = ops[o]
            if (not so.is_dma) and (not op.is_dma) and so.eng == "pe" and eng == "pe":
                return
            cur = deps.get(so.sem)
            if cur is None or o > cur:
                deps[so.sem] = o

        for k in reads:
            st = self.state.get(k)
            if st is None:
                st = self.state[k] = [{}, {}, False]
            for o in st[0].values():
                need(o)
        for k in writes:
            st = self.state.get(k)
            if st is None:
                st = self.state[k] = [{}, {}, False]
            for o in st[1].values():
                need(o)
            if not partial:
                for o in st[0].values():
                    need(o)
        for k in reads:
            st = self.state[k]
            if st[2]:
                st[1] = {}
                st[2] = False
            st[1][op.sem] = idx
        for k in writes:
            st = self.state[k]
            if (not partial) or (not st[2]):
                st[0] = {}
            st[0][op.sem] = idx
            st[2] = True
        for o in deps.values():
            ops[o].signal = True
        op.deps = deps
        ops.append(op)
        return idx

    def fence(self, keys):
        return self.add("sp", lambda e: e.nop(), reads=list(keys), writes=list(keys))

    def emit(self, final_waits=()):
        nc = self.nc
        for o in final_waits:
            self.ops[o].signal = True
        counters = {}
        for op in self.ops:
            if op.signal:
                c = counters.get(op.sem, 0) + (16 if op.is_dma else 1)
                counters[op.sem] = c
                op.ticket = c
        self.max_counts = dict(counters)
        semnames = sorted(counters.keys(), key=str)
        with contextlib.ExitStack() as es:
            handles = {}
            for s in semnames:
                handles[s] = es.enter_context(nc.semaphore("s_%s_%s" % (s[0], s[1])))
            block = es.enter_context(nc.Block())
            by_eng = {e: [] for e in self.ENG}
            for op in self.ops:
                by_eng[op.eng].append(op)
            allops = self.ops

            def run(e, name):
                waited = {}
                for op in by_eng[name]:
                    for s, o in op.deps.items():
                        t = allops[o].ticket
                        if waited.get(s, 0) >= t:
                            continue
                        e.wait_ge(handles[s], t)
                        waited[s] = t
                    ins = op.fn(e)
                    if op.signal:
                        ins.then_inc(handles[op.sem], 16 if op.is_dma else 1)
                if name == "sp":
                    for o in final_waits:
                        so = allops[o]
                        if waited.get(so.sem, 0) >= so.ticket:
                            continue
                        e.wait_ge(handles[so.sem], so.ticket)
                        waited[so.sem] = so.ticket

            @block.tensor
            def _(e):
                run(e, "pe")

            @block.scalar
            def _(e):
                run(e, "act")

            @block.vector
            def _(e):
                run(e, "dve")

            @block.gpsimd
            def _(e):
                run(e, "pool")

            @block.sync
            def _(e):
                run(e, "sp")


def _rel_bucket_np(n):
    n = np.maximum(n, 0)
    nf = np.maximum(n, 1).astype(np.float32)
    large = 16 + (np.log(nf / np.float32(16)) / np.float32(math.log(2048 / 16))
                  * np.float32(16)).astype(np.int32)
    large = np.minimum(large, 31)
    return np.where(n < 16, n, large)


def _onehots():
    def make(L, dil):
        m = np.arange(L)
        n = m - 127
        b = _rel_bucket_np(n)
        if dil:
            mult = ((n <= 128).astype(np.float32)
                    + ((n % 4 == 0) & (n <= 512)).astype(np.float32)
                    + ((n % 16 == 0) & (n <= 2048)).astype(np.float32))
        else:
            mult = np.ones(L, np.float32)
        mult = np.where(n >= 0, mult, 0.0).astype(np.float32)
        oh = np.zeros((32, L), np.float32)
        oh[b, m] = mult
        return oh
    return make(LA_DIL, True), make(LA_DIFF, False)


def build_program(n_layers=DEPTH, n_seq=2, debug=None, stop_after=None):
    debug = debug or ()
    nc = bass.Bass("TRN2", target_bir_lowering=False)

    def din(name, shape, dt=F32):
        return nc.dram_tensor(name, list(shape), dt, kind="ExternalInput").ap()

    def dscr(name, shape, dt):
        return nc.dram_tensor(name, list(shape), dt, kind="Internal").ap()

    x_in = din("x", [n_seq, S, D])
    g_attn = din("g_attn", [DEPTH, D])
    w_in = din("w_in", [DEPTH, D, 3 * D])
    w_out = din("w_out", [DEPTH, D, D])
    rel_bias = din("rel_bias", [32, 12])
    lambda_qk = din("lambda_qk", [DEPTH, 4, 64])
    subln_g = din("subln_g", [DEPTH, 128])
    g_ffn = din("g_ffn", [DEPTH, D])
    w_gu = din("w_gate_up", [DEPTH, D, 2 * DFF])
    w_down = din("w_down", [DEPTH, DFF, D])
    g_final = din("g_final", [1, D])
    oh_dil = din("oh_dil", [32, LA_DIL])
    oh_diff = din("oh_diff", [32, LA_DIFF])
    out = nc.dram_tensor("out", [n_seq, S, D], F32, kind="ExternalOutput").ap()

    xs = dscr("xs", [n_seq, S, D], F32)
    mixscr = dscr("mixscr", [n_seq, S, D], BF16)
    wb_in = dscr("wb_in", [DEPTH, 8, 128, 8 * 384], BF16)
    wb_out = dscr("wb_out", [DEPTH, 128, 8 * D], BF16)
    wb_gu = dscr("wb_gu", [DEPTH, NFC, 128, 8 * 256], BF16)
    wb_down = dscr("wb_down", [DEPTH, 128, NFC * D], BF16)
    a_dil = dscr("a_dil", [8, LA_DIL], F32)
    a_diff = dscr("a_diff", [4, LA_DIFF], F32)
    st_dil = dscr("st_dil", [8, 128, L_DIL], BF16)
    st_diff = dscr("st_diff", [4, 128, L_DIFF], BF16)

    dbg_out = {}

    def dbg_tensor(name, shape, dt):
        t = nc.dram_tensor("dbg_" + name, list(shape), dt, kind="ExternalOutput").ap()
        dbg_out[name] = t
        return t

    es = contextlib.ExitStack()
    with es:
        def sb(name, shape, dt):
            return es.enter_context(nc.sbuf_tensor(name, list(shape), dt))

        P = Prog(nc)
        finals = []

        R1 = sb("R1", [128, 32768], BF16)
        R2COLS = 54272
        R2 = sb("R2", [128, R2COLS], BF16)
        ident = sb("ident", [128, 128], BF16)
        identf = sb("identf", [128, 128], F32)
        Jf = sb("Jf", [128, 128], F32)
        zeros = sb("zeros", [128, 512], BF16)
        mhalf = sb("mhalf", [128, 4], F32)
        mhalf16 = sb("mhalf16", [128, 16], F32)
        sl4 = sb("sl4", [128, 16], F32)
        gA = sb("gA", [128, D], F32)
        gF = sb("gF", [128, D], F32)
        gFin = sb("gFin", [128, D], F32)
        sgain = sb("sgain", [128, 128], F32)
        lq = sb("lq", [128, 256], F32)
        lqp = sb("lqp", [128, 128], F32)
        lsm = sb("lsm", [128, 8], F32)
        neglam = sb("neglam", [128, 1], F32)
        expb = sb("expb", [32, 12], F32)
        relb = sb("relb", [32, 12], F32)
        junk = sb("junk", [128, D], BF16)
        ssb = [sb("ss%d" % i, [128, 1], F32) for i in range(4)]
        rsb = [sb("rs%d" % i, [128, 1], F32) for i in range(4)]

        banks = [es.enter_context(nc.psum_tensor("B%d" % i, [128, 512], F32)) for i in range(8)]

        def bankbf(i):
            return banks[i][:].bitcast(BF16)

        hT = R1[:, 0:32768].rearrange("p (c t) -> p c t", c=8)
        Wd = R1[:, 0:NFC * D].rearrange("p (c n) -> p c n", c=NFC)
        Wout = R1[:, NFC * D:NFC * D + 8 * D].rearrange("p (c n) -> p c n", c=8)
        R1_KEYS = ["hT", "Wd", "Wout", "pro1"]

        off = [0]

        def carve(ncols_bf16):
            a = off[0]
            off[0] += ncols_bf16
            assert off[0] <= R2COLS, off[0]
            return R2[:, a:a + ncols_bf16]

        QTa = carve(4096)
        QTb = carve(4096)
        KT = carve(4096)
        V1d = carve(32 * 130).rearrange("p (b c) -> p b c", b=32)
        V1f = carve(32 * 130).rearrange("p (b c) -> p b c", b=32)
        strips = [carve(5120) for _ in range(2)]
        Wp = [carve(8 * 384).rearrange("p (c n) -> p c n", c=8) for _ in range(2)]
        Ebuf = [carve(512) for _ in range(6)]
        Pbuf = [carve(512) for _ in range(6)]
        mixstage = carve(32 * 128).rearrange("p (b c) -> p b c", b=32)
        o1 = carve(1024).bitcast(F32).rearrange("p (a b) -> p a b", a=4)
        oo = carve(1024).bitcast(F32).rearrange("p (a b) -> p a b", a=4)
        osq = carve(1024).bitcast(F32).rearrange("p (a b) -> p a b", a=4)
        otmp = carve(256).bitcast(F32)
        rc = carve(16).bitcast(F32)
        ssq = carve(16).bitcast(F32)
        att_end = off[0]
        ATT_KEYS = ["QTa", "QTb", "KT", "V1d", "V1f", "strip0", "strip1", "Wp0", "Wp1",
                    "E0", "E1", "E2", "E3", "E4", "E5", "P0", "P1", "P2", "P3", "P4", "P5", "mixstage", "o1", "oo", "osq",
                    "otmp", "rc", "ssq"]

        off[0] = 0
        xt4s = [carve(8192).bitcast(F32).rearrange("p (b n) -> p b n", b=4) for _ in range(2)]
        h2b = [carve(1024) for _ in range(4)]
        mt = carve(4096).rearrange("p (b n) -> p b n", b=4)
        mixT = carve(4096).rearrange("p (c t) -> p c t", c=8)
        h2T = carve(4096).rearrange("p (c t) -> p c t", c=8)
        aT = carve(NFC * 512).rearrange("p (c t) -> p c t", c=NFC)
        Wgu = [carve(2048).rearrange("p (c n) -> p c n", c=8) for _ in range(4)]
        sgb = [carve(1024).bitcast(F32) for _ in range(2)]
        FFN_KEYS = ["xt4_0", "xt4_1", "h2b0", "h2b1", "h2b2", "h2b3", "mt", "mixT", "h2T", "aT", "Wgu0", "Wgu1", "Wgu2", "Wgu3", "sg0", "sg1"]
        off[0] = 0
        xb4 = [carve(2048).bitcast(F32) for _ in range(4)]
        hb4 = [carve(1024) for _ in range(4)]
        A0_KEYS = ["xb4_%d" % i for i in range(4)] + ["hb4_%d" % i for i in range(4)]
        R2_KEYS = ATT_KEYS + FFN_KEYS + A0_KEYS + ["pro2"]

        def mm(out, lhsT, rhs, start, stop, reads, writes, skip=False):
            if skip:
                P.add("pe", lambda e: e.matmul(out, lhsT=lhsT, rhs=rhs, start=start, stop=stop,
                                               skip_group_check=True),
                      reads=reads, writes=writes, partial=True)
            else:
                P.add("pe", lambda e: e.matmul(out, lhsT=lhsT, rhs=rhs, start=start, stop=stop),
                      reads=reads, writes=writes, partial=True)

        def dma(eng, out_ap, in_ap, reads, writes, dsem, partial=False):
            return P.add(eng, lambda e: e.dma_start(out=out_ap, in_=in_ap),
                         reads=reads, writes=writes, partial=partial, dsem=dsem)

        def bcast_rows(t2d, row, ncols, nparts=128):
            return bass.AP(t2d.tensor, t2d.offset + row * t2d.shape[-1], [[0, nparts], [1, ncols]])

        def conv_ops(l):
            ops_ = []
            key = ("wb", l)
            sem = "wb%d" % l
            win = w_in[l].rearrange("(kc p) n -> p kc n", p=128)
            for p in range(8):
                for which in range(3):
                    if p < 4:
                        cb = which * 512 + 128 * p
                    else:
                        cb = 1536 + which * 512 + 128 * (p - 4)
                    src = win[:, :, cb:cb + 128]
                    dst = wb_in[l, p].rearrange("p (kc w j) -> p kc w j", kc=8, w=3)[:, :, which, :]
                    ops_.append(lambda s=src, d=dst: dma("pool", d, s, [], [key], sem, partial=True))
            src = w_out[l].rearrange("(kc p) n -> p kc n", p=128)
            dst = wb_out[l].rearrange("p (kc n) -> p kc n", kc=8)
            ops_.append(lambda s=src, d=dst: dma("pool", d, s, [], [key], sem, partial=True))
            wgu = w_gu[l].rearrange("(kc p) (u c j) -> p kc u c j", p=128, u=2, j=128)
            for c in range(NFC):
                for u in range(2):
                    src = wgu[:, :, u, c, :]
                    dst = wb_gu[l, c].rearrange("p (kc u j) -> p kc u j", kc=8, u=2)[:, :, u, :]
                    ops_.append(lambda s=src, d=dst: dma("pool", d, s, [], [key], sem, partial=True))
            src = w_down[l].rearrange("(c p) n -> p c n", p=128)
            dst = wb_down[l].rearrange("p (c n) -> p c n", c=NFC)
            ops_.append(lambda s=src, d=dst: dma("pool", d, s, [], [key], sem, partial=True))
            return ops_

        P.add("pool", lambda e: e.memset(identf[:], 1.0), writes=["identf"])
        P.add("pool", lambda e: e.affine_select(out=identf[:], in_=identf[:], pattern=[[-1, 128]],
                                                compare_op=ALU.is_equal, fill=0.0, base=0,
                                                channel_multiplier=1),
              reads=["identf"], writes=["identf"])
        P.add("dve", lambda e: e.tensor_copy(out=ident[:], in_=identf[:]), reads=["identf"], writes=["ident"])
        P.add("pool", lambda e: e.memset(Jf[:], 1.0), writes=["Jf"])
        P.add("pool", lambda e: e.affine_select(out=Jf[:], in_=Jf[:], pattern=[[1, 128]],
                                                compare_op=ALU.is_equal, fill=0.0, base=-127,
                                                channel_multiplier=1),
              reads=["Jf"], writes=["Jf"])
        P.add("pool", lambda e: e.memset(zeros[:], 0.0), writes=["zeros"])
        P.add("pool", lambda e: e.memset(mhalf[:], -0.5), writes=["mhalf"])
        P.add("pool", lambda e: e.memset(mhalf16[:], -0.5), writes=["mhalf16"])

        for op_ in conv_ops(0):
            op_()
        pending_conv = []

        def drip_conv(n):
            for _ in range(min(n, len(pending_conv))):
                pending_conv.pop(0)()

        if stop_after != 'P0':
            ohd = R1[0:32, 0:2 * LA_DIL].bitcast(F32)
            ohf = R1[0:32, 2 * LA_DIL:2 * LA_DIL + 2 * LA_DIFF].bitcast(F32)
            arow = R2[0:8, 0:2 * LA_DIFF].bitcast(F32)
            dma("sp", relb[:], rel_bias, [], ["relb"], "relb")
            dma("sp", ohd, oh_dil, [], ["pro1"], "pro1", partial=True)
            dma("sp", ohf, oh_diff, [], ["pro1"], "pro1", partial=True)
            P.add("act", lambda e: e.activation(out=expb[:], in_=relb[:], func=AF.Exp),
                  reads=["relb"], writes=["expb"])
            for (nh, h0, LA, ohv, adst) in ((8, 0, LA_DIL, ohd, a_dil), (4, 8, LA_DIFF, ohf, a_diff)):
                nch = (LA + 511) // 512
                for ci in range(nch):
                    c0 = ci * 512
                    w_ = min(512, LA - c0)
                    bk = ci % 2
                    mm(banks[bk][0:nh, 0:w_], expb[:, h0:h0 + nh], ohv[:, c0:c0 + w_], True, True,
                       ["expb", "pro1"], ["B%d" % bk])
                    P.add("dve", lambda e, bk=bk, c0=c0, w_=w_, nh=nh: e.tensor_copy(
                        out=arow[0:nh, c0:c0 + w_], in_=banks[bk][0:nh, 0:w_]),
                        reads=["B%d" % bk], writes=["pro2"], partial=True)
                dma("sp", adst, arow[0:nh, 0:LA], ["pro2"], [("a", h0)], "pro2")
            P.fence(["pro1", "pro2"])
            Xv = R1[:, 0:2 * L_DIFF].bitcast(F32)
            Yv = R2[:, 0:L_DIFF]
            for h in range(12):
                if h < 8:
                    L, asrc, sdst = L_DIL, a_dil, st_dil[h]
                    hrow = h
                else:
                    L, asrc, sdst = L_DIFF, a_diff, st_diff[h - 8]
                    hrow = h - 8
                src = bass.AP(asrc.tensor, asrc.offset + hrow * asrc.shape[-1], [[1, 128], [1, L]])
                dma("sp", Xv[:, 0:L], src, [("a", 0 if h < 8 else 8)], ["pro1"], "pro1")
                for ci in range(L // 512):
                    bk = ci % 2
                    mm(banks[bk][:, :], Jf[:, :], Xv[:, ci * 512:(ci + 1) * 512], True, True,
                       ["Jf", "pro1"], ["B%d" % bk])
                    eng = "act" if ci % 2 else "dve"
                    if eng == "act":
                        P.add("act", lambda e, bk=bk, ci=ci: e.copy(out=Yv[:, ci * 512:(ci + 1) * 512], in_=banks[bk][:, :]),
                              reads=["B%d" % bk], writes=["pro2"], partial=True)
                    else:
                        P.add("dve", lambda e, bk=bk, ci=ci: e.tensor_copy(out=Yv[:, ci * 512:(ci + 1) * 512], in_=banks[bk][:, :]),
                              reads=["B%d" % bk], writes=["pro2"], partial=True)
                dma("sp", sdst, Yv[:, 0:L], ["pro2"], ["strips"], "pro2", partial=True)
            P.fence(R1_KEYS + R2_KEYS)

            if "strips" in debug:
                t = dbg_tensor("st_dil", [8, 128, L_DIL], BF16)
                finals.append(dma("sp", t, st_dil, ["strips"], ["dbg"], "dbg", partial=True))
                t = dbg_tensor("st_diff", [4, 128, L_DIFF], BF16)
                finals.append(dma("sp", t, st_diff, ["strips"], ["dbg"], "dbg", partial=True))


        def rms_to_bf16(x_ap, xkey, g_ap, gkey, i, out_ap, outkey):
            P.add("act", lambda e: e.activation(out=junk[:], in_=x_ap, func=AF.Square, accum_out=ssb[i][:]),
                  reads=[xkey], writes=["junk", "ss%d" % i])
            P.add("dve", lambda e: e.tensor_scalar(out=rsb[i][:], in0=ssb[i][:], scalar1=1.0 / D, scalar2=EPS,
                                                   op0=ALU.mult, op1=ALU.add),
                  reads=["ss%d" % i], writes=["rs%d" % i])
            P.add("pool", lambda e: e.tensor_tensor(out=rsb[i][:], in0=rsb[i][:], in1=mhalf[:, 0:1], op=ALU.pow),
                  reads=["rs%d" % i, "mhalf"], writes=["rs%d" % i])
            P.add("dve", lambda e: e.scalar_tensor_tensor(out=out_ap, in0=x_ap, scalar=rsb[i][:, 0:1], in1=g_ap,
                                                          op0=ALU.mult, op1=ALU.mult),
                  reads=[xkey, "rs%d" % i, gkey], writes=[outkey])

        tr_ctr = [0]

        def transpose_to(src_bf, srckey, dst3, dstkey, bank_list=(6, 7), evac="act"):
            bk = bank_list[tr_ctr[0] % len(bank_list)]
            tr_ctr[0] += 1
            bv = bankbf(bk)
            for c in range(8):
                P.add("pe", lambda e, c=c: e.transpose(bv[:, c * 128:(c + 1) * 128], src_bf[:, c * 128:(c + 1) * 128], ident[:]),
                      reads=[srckey, "ident"], writes=["B%d" % bk], partial=True)
            if evac == "act":
                P.add("act", lambda e: e.copy(out=dst3, in_=bv.rearrange("p (c t) -> p c t", c=8)),
                      reads=["B%d" % bk], writes=[dstkey], partial=True)
            else:
                P.add("dve", lambda e: e.tensor_copy(out=dst3, in_=bv.rearrange("p (c t) -> p c t", c=8)),
                      reads=["B%d" % bk], writes=[dstkey], partial=True)

        dma("sp", gFin[:], bcast_rows(g_final, 0, D), [], ["gFin"], "gFin")

        for l in range(n_layers if stop_after not in ('P0', 'P') else 0):
            lam_init = 0.8 - 0.6 * math.exp(-0.3 * l)
            dma("sp", gA[:], bcast_rows(g_attn, l, D), [], ["gA"], "gA")
            dma("sp", gF[:], bcast_rows(g_ffn, l, D), [], ["gF"], "gF")
            dma("sp", sgain[:], bcast_rows(subln_g, l, 128), [], ["sgain"], "sgain")
            lq_src = bass.AP(lambda_qk.tensor, lambda_qk.offset + l * 256, [[0, 128], [1, 256]])
            dma("sp", lq[:], lq_src, [], ["lq"], "lq")
            P.add("dve", lambda e, lam_init=lam_init: e.tensor_scalar(out=sgain[:], in0=sgain[:], scalar1=1.0 - lam_init, scalar2=None,
                                                   op0=ALU.mult),
                  reads=["sgain"], writes=["sgain"])
            lq4 = lq[:].rearrange("p (a b) -> p a b", a=4)
            lqp2 = lqp[:].rearrange("p (a b) -> p a b", a=2)
            P.add("dve", lambda e: e.tensor_tensor(out=lqp2[:, 0, :], in0=lq4[:, 0, :], in1=lq4[:, 1, :], op=ALU.mult),
                  reads=["lq"], writes=["lqp"], partial=True)
            P.add("dve", lambda e: e.tensor_tensor(out=lqp2[:, 1, :], in0=lq4[:, 2, :], in1=lq4[:, 3, :], op=ALU.mult),
                  reads=["lq"], writes=["lqp"], partial=True)
            P.add("dve", lambda e: e.reduce_sum(out=lsm[:, 0:2], in_=lqp2, axis=AX.X),
                  reads=["lqp"], writes=["lsm"])
            P.add("act", lambda e: e.activation(out=lsm[:, 2:4], in_=lsm[:, 0:2], func=AF.Exp),
                  reads=["lsm"], writes=["lsm2"])
            P.add("dve", lambda e: e.tensor_tensor(out=lsm[:, 4:5], in0=lsm[:, 3:4], in1=lsm[:, 2:3], op=ALU.subtract),
                  reads=["lsm2"], writes=["lsm3"])
            P.add("dve", lambda e, lam_init=lam_init: e.tensor_scalar(out=neglam[:], in0=lsm[:, 4:5], scalar1=-lam_init, scalar2=None,
                                                   op0=ALU.add),
                  reads=["lsm3"], writes=["neglam"])
            if l + 1 < n_layers:
                pending_conv.extend(conv_ops(l + 1))

            for s in range(n_seq):
                xsrc = x_in[s] if l == 0 else xs[s]
                last = (l == n_layers - 1)
                xdst = out[s] if last else xs[s]

                def pair_loads(p_):
                    wp_ = Wp[p_ % 2]
                    wkey_ = "Wp%d" % (p_ % 2)
                    dma("sp", wp_, wb_in[l, p_].rearrange("p (c n) -> p c n", c=8), [("wb", l)], [wkey_], wkey_)
                    stp_ = strips[p_ % 2]
                    skey_ = "strip%d" % (p_ % 2)
                    if p_ < 4:
                        dma("sp", stp_[:, 0:L_DIL], st_dil[2 * p_], ["strips"], [skey_], skey_, partial=True)
                        dma("sp", stp_[:, L_DIL:2 * L_DIL], st_dil[2 * p_ + 1], ["strips"], [skey_], skey_, partial=True)
                    else:
                        dma("sp", stp_[:, 0:L_DIFF], st_diff[p_ - 4], ["strips"], [skey_], skey_)

                pair_loads(0)
                for blk in range(NBLK + 2):
                    if blk < NBLK:
                        i = blk % 4
                        dma("sp", xb4[i][:, :], xsrc[blk * 128:(blk + 1) * 128, :], [("x", s, blk // 4)],
                            ["xb4_%d" % i], "xb4_%d" % i)
                        rms_to_bf16(xb4[i][:, :], "xb4_%d" % i, gA[:], "gA", i, hb4[i][:, :], "hb4_%d" % i)
                    if blk >= 2:
                        b2 = blk - 2
                        i2 = b2 % 4
                        transpose_to(hb4[i2], "hb4_%d" % i2, hT[:, :, b2 * 128:(b2 + 1) * 128], "hT",
                                     bank_list=(4, 5, 6, 7), evac=("act" if b2 % 2 == 0 else "dve"))
                P.fence(R2_KEYS)
                if stop_after == "A0":
                    break

                P.add("pool", lambda e: e.memset(V1d[:, :, 64:65], 1.0), writes=["V1d"], partial=True)
                P.add("pool", lambda e: e.memset(V1d[:, :, 129:130], 1.0), writes=["V1d"], partial=True)
                P.add("pool", lambda e: e.memset(V1f[:, :, 128:129], 1.0), writes=["V1f"], partial=True)
                P.add("pool", lambda e: e.memset(QTa[64:128, :], 0.0), writes=["QTa"], partial=True)
                P.add("pool", lambda e: e.memset(QTb[0:64, :], 0.0), writes=["QTb"], partial=True)
                tile_ctr = [0]
                ob_ctr = [0]
                for p in range(8):
                    is_dil = p < 4
                    wp = Wp[p % 2]
                    wkey = "Wp%d" % (p % 2)
                    stp = strips[p % 2]
                    skey = "strip%d" % (p % 2)
                    if p + 1 < 8:
                        pair_loads(p + 1)
                    pj = 0
                    for tt in range(8):
                        for which in (0, 1):
                            bk = (7, 0, 1, 2)[pj % 4]
                            pj += 1
                            for kc in range(8):
                                mm(banks[bk][:, :], wp[:, kc, which * 128:(which + 1) * 128],
                                   hT[:, kc, tt * 512:(tt + 1) * 512], kc == 0, kc == 7,
                                   [wkey, "hT"], ["B%d" % bk])
                            if which == 0:
                                P.add("act", lambda e, bk=bk, tt=tt: e.activation(
                                    out=QTa[0:64, tt * 512:(tt + 1) * 512], in_=banks[bk][0:64, :], func=AF.Copy, scale=0.125),
                                    reads=["B%d" % bk], writes=["QTa"], partial=True)
                                P.add("dve", lambda e, bk=bk, tt=tt: e.tensor_scalar(
                                    out=QTb[64:128, tt * 512:(tt + 1) * 512], in0=banks[bk][64:128, :], scalar1=0.125,
                                    scalar2=None, op0=ALU.mult),
                                    reads=["B%d" % bk], writes=["QTb"], partial=True)
                            else:
                                P.add("dve", lambda e, bk=bk, tt=tt: e.tensor_copy(
                                    out=KT[:, tt * 512:(tt + 1) * 512], in_=banks[bk][:, :]),
                                    reads=["B%d" % bk], writes=["KT"], partial=True)
                    for g in range(8):
                        bk = (7, 0, 1, 2)[pj % 4]
                        pj += 1
                        for b4 in range(4):
                            blk = 4 * g + b4
                            for kc in range(8):
                                mm(banks[bk][:, b4 * 128:(b4 + 1) * 128], hT[:, kc, blk * 128:(blk + 1) * 128],
                                   wp[:, kc, 256:384], kc == 0, kc == 7, [wkey, "hT"], ["B%d" % bk])
                        if is_dil:
                            dstv = V1d[:, 4 * g:4 * g + 4, :].rearrange("p b (h c) -> p b h c", h=2)[:, :, :, 0:64]
                            srcv = banks[bk][:, :].rearrange("p (b h c) -> p b h c", b=4, h=2)
                            eng = "act" if g % 2 else "dve"
                            if eng == "act":
                                P.add("act", lambda e, d=dstv, s_=srcv: e.copy(out=d, in_=s_),
                                      reads=["B%d" % bk], writes=["V1d"], partial=True)
                            else:
                                P.add("dve", lambda e, d=dstv, s_=srcv: e.tensor_copy(out=d, in_=s_),
                                      reads=["B%d" % bk], writes=["V1d"], partial=True)
                        else:
                            dstv = V1f[:, 4 * g:4 * g + 4, 0:128]
                            srcv = banks[bk][:, :].rearrange("p (b c) -> p b c", b=4)
                            eng = "act" if g % 2 else "dve"
                            if eng == "act":
                                P.add("act", lambda e, d=dstv, s_=srcv: e.copy(out=d, in_=s_),
                                      reads=["B%d" % bk], writes=["V1f"], partial=True)
                            else:
                                P.add("dve", lambda e, d=dstv, s_=srcv: e.tensor_copy(out=d, in_=s_),
                                      reads=["B%d" % bk], writes=["V1f"], partial=True)
                    if "qkv" in debug and l == 0 and s == 0 and p in (0, 4):
                        t = dbg_tensor("QT%d" % p, [128, 4096], BF16)
                        finals.append(dma("sp", t, QTa, ["QTa"], ["dbg"], "dbgq", partial=True))
                        t = dbg_tensor("KT%d" % p, [128, 4096], BF16)
                        finals.append(dma("sp", t, KT, ["KT"], ["dbg"], "dbgq", partial=True))
                        t = dbg_tensor("V%d" % p, [128, 32, 130], BF16)
                        vv = V1d if is_dil else V1f
                        finals.append(dma("sp", t, vv, ["V1d" if is_dil else "V1f"], ["dbg"], "dbgq", partial=True))

                    if stop_after == "A1":
                        break
                    if p == 7:
                        P.fence(R1_KEYS)
                        dma("sp", Wd, wb_down[l].rearrange("p (c n) -> p c n", c=NFC), [("wb", l)], ["Wd"], "Wd")
                        dma("sp", Wout, wb_out[l].rearrange("p (c n) -> p c n", c=8), [("wb", l)], ["Wout"], "Wout")
                    dv = 64 if is_dil else 128
                    accw = dv + 1
                    vkey = "V1d" if is_dil else "V1f"
                    V1 = V1d if is_dil else V1f
                    pend = []

                    def make_pv(j, m0, m1, pi, Oa, Ob, vcol0, is_last):
                        def f():
                            for qb in range(m0, m1):
                                bk = Oa if (is_dil or qb < 2) else Ob
                                c0 = (qb if is_dil else qb % 2) * accw
                                mm(banks[bk][:, c0:c0 + accw], Pbuf[pi][:, qb * 128:(qb + 1) * 128],
                                   V1[:, j, vcol0:vcol0 + accw], False, is_last,
                                   ["P%d" % pi, vkey], ["B%d" % bk], skip=True)
                        return f

                    def make_norm(t, v, Oa, Ob):
                        def Uview(qb):
                            bk = Oa if (is_dil or qb < 2) else Ob
                            c0 = (qb if is_dil else qb % 2) * accw
                            return banks[bk][:, c0:c0 + dv], banks[bk][:, c0 + dv:c0 + dv + 1], bk

                        def f():
                            for qb in range(4):
                                U, sden, bk = Uview(qb)
                                P.add("dve", lambda e, sden=sden, qb=qb: e.reciprocal(out=rc[:, qb:qb + 1], in_=sden),
                                      reads=["B%d" % bk], writes=["rc"], partial=True)
                            if is_dil:
                                for qb in range(4):
                                    U, sden, bk = Uview(qb)
                                    P.add("dve", lambda e, U=U, qb=qb: e.tensor_scalar(
                                        out=mixstage[:, 4 * t + qb, 64 * v:64 * v + 64], in0=U,
                                        scalar1=rc[:, qb:qb + 1], scalar2=None, op0=ALU.mult),
                                        reads=["B%d" % bk, "rc"], writes=["mixstage"], partial=True)
                            elif v == 0:
                                for qb in range(4):
                                    U, sden, bk = Uview(qb)
                                    P.add("dve", lambda e, U=U, qb=qb: e.tensor_scalar(
                                        out=o1[:, qb, :], in0=U, scalar1=rc[:, qb:qb + 1], scalar2=None, op0=ALU.mult),
                                        reads=["B%d" % bk, "rc"], writes=["o1"], partial=True)
                            else:
                                P.add("dve", lambda e: e.tensor_scalar(out=ssq[:, 0:4], in0=rc[:, 0:4], scalar1=neglam[:, 0:1],
                                                                       scalar2=None, op0=ALU.mult),
                                      reads=["rc", "neglam"], writes=["ssq"])
                                for qb in range(4):
                                    U, sden, bk = Uview(qb)
                                    P.add("dve", lambda e, U=U, qb=qb: e.scalar_tensor_tensor(
                                        out=mixstage[:, 4 * t + qb, :], in0=U, scalar=ssq[:, qb:qb + 1], in1=o1[:, qb, :],
                                        op0=ALU.mult, op1=ALU.add),
                                        reads=["B%d" % bk, "ssq", "o1"], writes=["mixstage"], partial=True)
                        return f

                    import os as _os
                    _tmax = int(_os.environ.get("K_TMAX", "8"))
                    _vset = [int(c) for c in _os.environ.get("K_VSET", "01")]
                    for t in range(_tmax):
                        for v in _vset:
                            pb = 64 * v
                            if is_dil:
                                strip = stp[:, v * L_DIL:(v + 1) * L_DIL]
                                vcol0 = 65 * v
                                jlo = max(0, 4 * t - 16)
                            else:
                                strip = stp[:, 0:L_DIFF]
                                vcol0 = 0
                                jlo = 0
                            oset = ob_ctr[0] % 2
                            ob_ctr[0] += 1
                            if is_dil:
                                Oa, Ob = 3 + oset, 3 + oset
                                zb = (Oa,)
                            else:
                                Oa, Ob = 3 + 2 * oset, 4 + 2 * oset
                                zb = (Oa, Ob)
                            zw = (4 if is_dil else 2) * accw
                            for bk in zb:
                                mm(banks[bk][:, 0:zw], zeros[:, 0:128], zeros[:, 0:zw], True, False,
                                   ["zeros"], ["B%d" % bk], skip=True)
                            jl = list(range(jlo, 4 * t + 4))
                            for j in jl:
                                m0 = max(0, j - 4 * t)
                                c0 = 128 * m0
                                D0 = 512 * t - 128 * j
                                c1 = min(512, 2176 - D0) if is_dil else 512
                                m1 = c1 // 128
                                ti = tile_ctr[0]
                                tile_ctr[0] += 1
                                if is_dil:
                                    sbk = (0, 1, 2, 7, 5, 6)[ti % 6]
                                    pi = ti % 6
                                else:
                                    sbk = (0, 1, 2, 7)[ti % 4]
                                    pi = ti % 4
                                QTv = QTb if v else QTa
                                mm(banks[sbk][:, c0:c1], KT[:, j * 128:(j + 1) * 128],
                                   QTv[:, 512 * t + c0:512 * t + c1], True, True,
                                   ["KT", "QTb" if v else "QTa"], ["B%d" % sbk])
                                P.add("act", lambda e, sbk=sbk, pi=pi, c0=c0, c1=c1: e.activation(
                                    out=Ebuf[pi][:, c0:c1], in_=banks[sbk][:, c0:c1], func=AF.Exp),
                                    reads=["B%d" % sbk], writes=["E%d" % pi])
                                P.add("dve", lambda e, pi=pi, c0=c0, c1=c1, D0=D0, strip=strip: e.tensor_tensor(
                                    out=Pbuf[pi][:, c0:c1], in0=Ebuf[pi][:, c0:c1],
                                    in1=strip[:, D0 + c0:D0 + c1], op=ALU.mult),
                                    reads=["E%d" % pi, skey], writes=["P%d" % pi])
                                pend.append(make_pv(j, m0, m1, pi, Oa, Ob, vcol0, j == jl[-1]))
                                if j == jl[-1]:
                                    pend.append(make_norm(t, v, Oa, Ob))
                                while len(pend) > (5 if is_dil else 3):
                                    pend.pop(0)()
                    while pend:
                        pend.pop(0)()
                    mdst = mixscr[s].rearrange("(b p) c -> p b c", p=128)[:, :, p * 128:(p + 1) * 128]
                    dma("sp", mdst, mixstage, ["mixstage"], [("mix", s, tt_) for tt_ in range(8)], "mixstage",
                        partial=True)
                    drip_conv(9)
                    if stop_after == "B1" and p == 0:
                        break
                if stop_after in ("B", "B1", "A1"):
                    break

                P.fence(R2_KEYS)
                pcnt = [0]

                def pbank():
                    b_ = pcnt[0] % 4
                    pcnt[0] += 1
                    return b_

                def c_loads(tt):
                    r0 = tt * 512
                    xk = "xt4_%d" % (tt % 2)
                    dma("pool", mt, mixscr[s, r0:r0 + 512, :].rearrange("(b p) c -> p b c", p=128),
                        [("mix", s, tt)], ["mt"], "mt")
                    dma("pool", xt4s[tt % 2], xsrc[r0:r0 + 512, :].rearrange("(b p) c -> p b c", p=128),
                        [("x", s, tt)], [xk], xk)

                def c_subln(tt):
                    junkf = junk[:, :].bitcast(F32)
                    for b in range(4):
                        mtd = mt[:, b, 512:1024]
                        P.add("dve", lambda e, mtd=mtd: e.tensor_tensor(out=junkf, in0=mtd, in1=mtd, op=ALU.mult),
                              reads=["mt"], writes=["junk"])
                        P.add("dve", lambda e, b=b: e.reduce_sum(out=sl4[:, 4 * b:4 * b + 4],
                                                                 in_=junkf.rearrange("p (h c) -> p h c", h=4), axis=AX.X),
                              reads=["junk"], writes=["sl4"], partial=True)
                    P.add("dve", lambda e: e.tensor_scalar(out=sl4[:, 0:16], in0=sl4[:, 0:16], scalar1=1.0 / 128,
                                                           scalar2=SUBLN_EPS, op0=ALU.mult, op1=ALU.add),
                          reads=["sl4"], writes=["sl4"])
                    P.add("pool", lambda e: e.tensor_tensor(out=sl4[:, 0:16], in0=sl4[:, 0:16], in1=mhalf16[:, 0:16], op=ALU.pow),
                          reads=["sl4", "mhalf16"], writes=["sl4"])
                    for b in range(4):
                        for h in range(4):
                            mv = mt[:, b, 512 + 128 * h:512 + 128 * (h + 1)]
                            P.add("dve", lambda e, mv=mv, b=b, h=h: e.scalar_tensor_tensor(
                                out=mv, in0=mv, scalar=sl4[:, 4 * b + h:4 * b + h + 1], in1=sgain[:],
                                op0=ALU.mult, op1=ALU.mult),
                                reads=["mt", "sl4", "sgain"], writes=["mt"], partial=True)

                def c_pe1(tt):
                    xt4 = xt4s[tt % 2]
                    xk = "xt4_%d" % (tt % 2)
                    for b in range(4):
                        transpose_to(mt[:, b, :], "mt", mixT[:, :, b * 128:(b + 1) * 128], "mixT",
                                     bank_list=(4, 5, 6, 7), evac=("act" if b % 2 == 0 else "dve"))
                    for b in range(4):
                        for half in range(2):
                            bk = pbank()
                            for kc in range(8):
                                mm(banks[bk][:, :], mixT[:, kc, b * 128:(b + 1) * 128],
                                   Wout[:, kc, half * 512:(half + 1) * 512], kc == 0, kc == 7,
                                   ["mixT", "Wout"], ["B%d" % bk])
                            P.add("dve", lambda e, bk=bk, b=b, half=half, xt4=xt4: e.tensor_tensor(
                                out=xt4[:, b, half * 512:(half + 1) * 512], in0=banks[bk][:, :],
                                in1=xt4[:, b, half * 512:(half + 1) * 512], op=ALU.add),
                                reads=["B%d" % bk, xk], writes=[xk], partial=True)

                def c_norm_elem(tt):
                    xt4 = xt4s[tt % 2]
                    xk = "xt4_%d" % (tt % 2)
                    for b in range(4):
                        rms_to_bf16(xt4[:, b, :], xk, gF[:], "gF", b, h2b[b][:, :], "h2b%d" % b)

                def c_norm_T(tt):
                    for b in range(4):
                        transpose_to(h2b[b], "h2b%d" % b, h2T[:, :, b * 128:(b + 1) * 128], "h2T",
                                     bank_list=(4, 5, 6, 7), evac=("act" if b % 2 == 0 else "dve"))

                def c_gu(tt):
                    for c in range(NFC):
                        wi = c % 4
                        wk = "Wgu%d" % wi
                        dma("sp", Wgu[wi], wb_gu[l, c].rearrange("p (c n) -> p c n", c=8), [("wb", l)], [wk], wk)
                        bg = pbank()
                        for kc in range(8):
                            mm(banks[bg][:, :], Wgu[wi][:, kc, 0:128], h2T[:, kc, :], kc == 0, kc == 7,
                               [wk, "h2T"], ["B%d" % bg])
                        bu = pbank()
                        for kc in range(8):
                            mm(banks[bu][:, :], Wgu[wi][:, kc, 128:256], h2T[:, kc, :], kc == 0, kc == 7,
                               [wk, "h2T"], ["B%d" % bu])
                        si = c % 2
                        P.add("act", lambda e, bg=bg, si=si: e.activation(out=sgb[si][:, :], in_=banks[bg][:, :], func=AF.Silu),
                              reads=["B%d" % bg], writes=["sg%d" % si])
                        P.add("dve", lambda e, bu=bu, si=si, c=c: e.tensor_tensor(
                            out=aT[:, c, :], in0=sgb[si][:, :], in1=banks[bu][:, :], op=ALU.mult),
                            reads=["sg%d" % si, "B%d" % bu], writes=["aT"], partial=True)

                def c_down(tt):
                    r0 = tt * 512
                    xt4 = xt4s[tt % 2]
                    xk = "xt4_%d" % (tt % 2)
                    for b in range(4):
                        for half in range(2):
                            bk = pbank()
                            for c in range(NFC):
                                mm(banks[bk][:, :], aT[:, c, b * 128:(b + 1) * 128],
                                   Wd[:, c, half * 512:(half + 1) * 512], c == 0, c == NFC - 1,
                                   ["aT", "Wd"], ["B%d" % bk])
                            P.add("dve", lambda e, bk=bk, b=b, half=half, xt4=xt4: e.tensor_tensor(
                                out=xt4[:, b, half * 512:(half + 1) * 512], in0=banks[bk][:, :],
                                in1=xt4[:, b, half * 512:(half + 1) * 512], op=ALU.add),
                                reads=["B%d" % bk, xk], writes=[xk], partial=True)
                    if last:
                        for b in range(4):
                            i = b
                            xb_ = xt4[:, b, :]
                            P.add("act", lambda e, xb_=xb_, i=i: e.activation(out=junk[:], in_=xb_, func=AF.Square,
                                                                             accum_out=ssb[i][:]),
                                  reads=[xk], writes=["junk", "ss%d" % i])
                            P.add("dve", lambda e, i=i: e.tensor_scalar(out=rsb[i][:], in0=ssb[i][:], scalar1=1.0 / D,
                                                                        scalar2=EPS, op0=ALU.mult, op1=ALU.add),
                                  reads=["ss%d" % i], writes=["rs%d" % i])
                            P.add("pool", lambda e, i=i: e.tensor_tensor(out=rsb[i][:], in0=rsb[i][:], in1=mhalf[:, 0:1],
                                                                         op=ALU.pow),
                                  reads=["rs%d" % i, "mhalf"], writes=["rs%d" % i])
                            P.add("dve", lambda e, xb_=xb_, i=i: e.scalar_tensor_tensor(
                                out=xb_, in0=xb_, scalar=rsb[i][:, 0:1], in1=gFin[:], op0=ALU.mult, op1=ALU.mult),
                                reads=[xk, "rs%d" % i, "gFin"], writes=[xk], partial=True)
                    st = dma("sp", xdst[r0:r0 + 512, :].rearrange("(b p) c -> p b c", p=128), xt4,
                             [xk], [("x", s, tt)], xk)
                    if last:
                        finals.append(st)

                c_loads(0)
                c_subln(0)
                c_pe1(0)
                c_norm_elem(0)
                c_norm_T(0)
                for tt in range(8):
                    if tt + 1 < 8:
                        c_loads(tt + 1)
                        c_subln(tt + 1)
                    c_gu(tt)
                    if tt + 1 < 8:
                        c_pe1(tt + 1)
                        c_norm_elem(tt + 1)
                    c_down(tt)
                    if tt + 1 < 8:
                        c_norm_T(tt + 1)
                    drip_conv(3)
                P.fence(R1_KEYS + R2_KEYS)
                drip_conv(1000)
            if stop_after is not None:
                break

        if "hT" in debug:
            t = dbg_tensor("hT", [128, 8, 4096], BF16)
            finals.append(dma("sp", t, hT, ["hT"], ["dbg"], "dbgh"))
        if "mix" in debug:
            t = dbg_tensor("mix", [n_seq, S, D], BF16)
            finals.append(dma("sp", t, mixscr, [("mix", s_, tt_) for s_ in range(n_seq) for tt_ in range(8)],
                              ["dbg"], "dbgm"))
        if "xs" in debug:
            t = dbg_tensor("xs", [n_seq, S, D], F32)
            finals.append(dma("sp", t, xs, [("x", s_, tt_) for s_ in range(n_seq) for tt_ in range(8)],
                              ["dbg"], "dbgx"))
        if not finals:
            finals.append(P.fence(R1_KEYS + R2_KEYS))
        P.emit(final_waits=finals)
    nc._dbg_out = dbg_out
    nc._prog = P
    return nc


_ARG_ORDER = ("g_attn", "w_in", "w_out", "rel_bias", "lambda_qk", "subln_g", "g_ffn",
              "w_gate_up", "w_down", "g_final")


def make_in_maps(inputs, n_cores=8, n_seq=2):
    ohd, ohf = _onehots()
    shared = {}
    for k in _ARG_ORDER:
        a = np.ascontiguousarray(np.asarray(inputs[k], dtype=np.float32))
        if k == "g_final":
            a = a.reshape(1, D)
        shared[k] = a
    shared["oh_dil"] = ohd
    shared["oh_diff"] = ohf
    x = np.asarray(inputs["x"], dtype=np.float32)
    maps = []
    for c in range(n_cores):
        m = dict(shared)
        m["x"] = np.ascontiguousarray(x[c * n_seq:(c + 1) * n_seq])
        maps.append(m)
    return maps


def kernel(x, g_attn, w_in, w_out, rel_bias, lambda_qk, subln_g, g_ffn, w_gate_up, w_down, g_final):
    inputs = dict(x=x, g_attn=g_attn, w_in=w_in, w_out=w_out, rel_bias=rel_bias, lambda_qk=lambda_qk,
                  subln_g=subln_g, g_ffn=g_ffn, w_gate_up=w_gate_up, w_down=w_down, g_final=g_final)
    nc = build_program()
    in_maps = make_in_maps(inputs)
    res = run_bass_kernel_spmd(nc, in_maps, core_ids=list(range(8)))
    outs = [np.asarray(r["out"]) for r in res.results]
    return np.concatenate(outs, axis=0).astype(np.float32)
```

```python
import contextlib
import math

import numpy as np
import concourse.bass as bass
import concourse.mybir as mybir
from concourse.bass_utils import run_bass_kernel_spmd

F32 = mybir.dt.float32
BF16 = mybir.dt.bfloat16
ALU = mybir.AluOpType
AF = mybir.ActivationFunctionType
AX = mybir.AxisListType

D = 1024
S = 4096
DEPTH = 4
DFF = 2816
NFC = DFF // 128
NBLK = S // 128
EPS = 1e-6
SUBLN_EPS = 1e-5
L_DIL = 2560
L_DIFF = 4096
LA_DIL = L_DIL + 128
LA_DIFF = L_DIFF + 128


class _Op:
    __slots__ = ("eng", "fn", "deps", "sem", "signal", "ticket", "is_dma")


class Prog:
    ENG = ("pe", "act", "dve", "pool", "sp")

    def __init__(self, nc):
        self.nc = nc
        self.ops = []
        self.state = {}

    def add(self, eng, fn, reads=(), writes=(), partial=False, dsem=None):
        idx = len(self.ops)
        op = _Op()
        op.eng = eng
        op.fn = fn
        op.is_dma = dsem is not None
        op.sem = ("d", dsem) if op.is_dma else ("e", eng)
        op.signal = op.is_dma
        op.ticket = None
        deps = {}
        ops = self.ops

        def need(o):
            so = ops[o]
            if (not so.is_dma) and (not op.is_dma) and so.eng == "pe" and eng == "pe":
                return
            cur = deps.get(so.sem)
            if cur is None or o > cur:
                deps[so.sem] = o

        for k in reads:
            st = self.state.get(k)
            if st is None:
                st = self.state[k] = [{}, {}, False]
            for o in st[0].values():
                need(o)
        for k in writes:
            st = self.state.get(k)
            if st is None:
                st = self.state[k] = [{}, {}, False]
            for o in st[1].values():
                need(o)
            if not partial:
                for o in st[0].values():
                    need(o)
        for k in reads:
            st = self.state[k]
            if st[2]:
                st[1] = {}
                st[2] = False
            st[1][op.sem] = idx
        for k in writes:
            st = self.state[k]
            if (not partial) or (not st[2]):
                st[0] = {}
            st[0][op.sem] = idx
            st[2] = True
        for o in deps.values():
            ops[o].signal = True
        op.deps = deps
        ops.append(op)
        return idx

    def fence(self, keys):
        return self.add("sp", lambda e: e.nop(), reads=list(keys), writes=list(keys))

    def emit(self, final_waits=()):
        nc = self.nc
        for o in final_waits:
            self.ops[o].signal = True
        counters = {}
        for op in self.ops:
            if op.signal:
                c = counters.get(op.sem, 0) + (16 if op.is_dma else 1)
                counters[op.sem] = c
                op.ticket = c
        self.max_counts = dict(counters)
        semnames = sorted(counters.keys(), key=str)
        with contextlib.ExitStack() as es:
            handles = {}
            for s in semnames:
                handles[s] = es.enter_context(nc.semaphore("s_%s_%s" % (s[0], s[1])))
            block = es.enter_context(nc.Block())
            by_eng = {e: [] for e in self.ENG}
            for op in self.ops:
                by_eng[op.eng].append(op)
            allops = self.ops

            def run(e, name):
                waited = {}
                for op in by_eng[name]:
                    for s, o in op.deps.items():
                        t = allops[o].ticket
                        if waited.get(s, 0) >= t:
                            continue
                        e.wait_ge(handles[s], t)
                        waited[s] = t
                    ins = op.fn(e)
                    if op.signal:
                        ins.then_inc(handles[op.sem], 16 if op.is_dma else 1)
                if name == "sp":
                    for o in final_waits:
                        so = allops[o]
                        if waited.get(so.sem, 0) >= so.ticket:
                            continue
                        e.wait_ge(handles[so.sem], so.ticket)
                        waited[so.sem] = so.ticket

            @block.tensor
            def _(e):
                run(e, "pe")

            @block.scalar
            def _(e):
                run(e, "act")

            @block.vector
            def _(e):
                run(e, "dve")

            @block.gpsimd
            def _(e):
                run(e, "pool")

            @block.sync
            def _(e):
                run(e, "sp")


def _rel_bucket_np(n):
    n = np.maximum(n, 0)
    nf = np.maximum(n, 1).astype(np.float32)
    large = 16 + (np.log(nf / np.float32(16)) / np.float32(math.log(2048 / 16))
                  * np.float32(16)).astype(np.int32)
    large = np.minimum(large, 31)
    return np.where(n < 16, n, large)


def _onehots():
    def make(L, dil):
        m = np.arange(L)
        n = m - 127
        b = _rel_bucket_np(n)
        if dil:
            mult = ((n <= 128).astype(np.float32)
                    + ((n % 4 == 0) & (n <= 512)).astype(np.float32)
                    + ((n % 16 == 0) & (n <= 2048)).astype(np.float32))
        else:
            mult = np.ones(L, np.float32)
        mult = np.where(n >= 0, mult, 0.0).astype(np.float32)
        oh = np.zeros((32, L), np.float32)
        oh[b, m] = mult
        return oh
    return make(LA_DIL, True), make(LA_DIFF, False)


def build_program(n_layers=DEPTH, n_seq=2, debug=None, stop_after=None):
    debug = debug or ()
    nc = bass.Bass("TRN2", target_bir_lowering=False)

    def din(name, shape, dt=F32):
        return nc.dram_tensor(name, list(shape), dt, kind="ExternalInput").ap()

    def dscr(name, shape, dt):
        return nc.dram_tensor(name, list(shape), dt, kind="Internal").ap()

    x_in = din("x", [n_seq, S, D])
    g_attn = din("g_attn", [DEPTH, D])
    w_in = din("w_in", [DEPTH, D, 3 * D])
    w_out = din("w_out", [DEPTH, D, D])
    rel_bias = din("rel_bias", [32, 12])
    lambda_qk = din("lambda_qk", [DEPTH, 4, 64])
    subln_g = din("subln_g", [DEPTH, 128])
    g_ffn = din("g_ffn", [DEPTH, D])
    w_gu = din("w_gate_up", [DEPTH, D, 2 * DFF])
    w_down = din("w_down", [DEPTH, DFF, D])
    g_final = din("g_final", [1, D])
    oh_dil = din("oh_dil", [32, LA_DIL])
    oh_diff = din("oh_diff", [32, LA_DIFF])
    out = nc.dram_tensor("out", [n_seq, S, D], F32, kind="ExternalOutput").ap()

    xs = dscr("xs", [n_seq, S, D], F32)
    mixscr = dscr("mixscr", [n_seq, S, D], BF16)
    wb_in = dscr("wb_in", [DEPTH, 8, 128, 8 * 384], BF16)
    wb_out = dscr("wb_out", [DEPTH, 128, 8 * D], BF16)
    wb_gu = dscr("wb_gu", [DEPTH, NFC, 128, 8 * 256], BF16)
    wb_down = dscr("wb_down", [DEPTH, 128, NFC * D], BF16)
    a_dil = dscr("a_dil", [8, LA_DIL], F32)
    a_diff = dscr("a_diff", [4, LA_DIFF], F32)
    st_dil = dscr("st_dil", [8, 128, L_DIL], BF16)
    st_diff = dscr("st_diff", [4, 128, L_DIFF], BF16)

    dbg_out = {}

    def dbg_tensor(name, shape, dt):
        t = nc.dram_tensor("dbg_" + name, list(shape), dt, kind="ExternalOutput").ap()
        dbg_out[name] = t
        return t

    es = contextlib.ExitStack()
    with es:
        def sb(name, shape, dt):
            return es.enter_context(nc.sbuf_tensor(name, list(shape), dt))

        P = Prog(nc)
        finals = []

        R1 = sb("R1", [128, 32768], BF16)
        R2COLS = 54272
        R2 = sb("R2", [128, R2COLS], BF16)
        ident = sb("ident", [128, 128], BF16)
        identf = sb("identf", [128, 128], F32)
        Jf = sb("Jf", [128, 128], F32)
        zeros = sb("zeros", [128, 512], BF16)
        mhalf = sb("mhalf", [128, 4], F32)
        mhalf16 = sb("mhalf16", [128, 16], F32)
        sl4 = sb("sl4", [128, 16], F32)
        gA = sb("gA", [128, D], F32)
        gF = sb("gF", [128, D], F32)
        gFin = sb("gFin", [128, D], F32)
        sgain = sb("sgain", [128, 128], F32)
        lq = sb("lq", [128, 256], F32)
        lqp = sb("lqp", [128, 128], F32)
        lsm = sb("lsm", [128, 8], F32)
        neglam = sb("neglam", [128, 1], F32)
        expb = sb("expb", [32, 12], F32)
        relb = sb("relb", [32, 12], F32)
        junk = sb("junk", [128, D], BF16)
        ssb = [sb("ss%d" % i, [128, 1], F32) for i in range(4)]
        rsb = [sb("rs%d" % i, [128, 1], F32) for i in range(4)]

        banks = [es.enter_context(nc.psum_tensor("B%d" % i, [128, 512], F32)) for i in range(8)]

        def bankbf(i):
            return banks[i][:].bitcast(BF16)

        hT = R1[:, 0:32768].rearrange("p (c t) -> p c t", c=8)
        Wd = R1[:, 0:NFC * D].rearrange("p (c n) -> p c n", c=NFC)
        Wout = R1[:, NFC * D:NFC * D + 8 * D].rearrange("p (c n) -> p c n", c=8)
        R1_KEYS = ["hT", "Wd", "Wout", "pro1"]

        off = [0]

        def carve(ncols_bf16):
            a = off[0]
            off[0] += ncols_bf16
            assert off[0] <= R2COLS, off[0]
            return R2[:, a:a + ncols_bf16]

        QTa = carve(4096)
        QTb = carve(4096)
        KT = carve(4096)
        V1d = carve(32 * 130).rearrange("p (b c) -> p b c", b=32)
        V1f = carve(32 * 130).rearrange("p (b c) -> p b c", b=32)
        strips = [carve(5120) for _ in range(2)]
        Wp = [carve(8 * 384).rearrange("p (c n) -> p c n", c=8) for _ in range(2)]
        Ebuf = [carve(512) for _ in range(6)]
        Pbuf = [carve(512) for _ in range(6)]
        mixstage = carve(32 * 128).rearrange("p (b c) -> p b c", b=32)
        o1 = carve(1024).bitcast(F32).rearrange("p (a b) -> p a b", a=4)
        oo = carve(1024).bitcast(F32).rearrange("p (a b) -> p a b", a=4)
        osq = carve(1024).bitcast(F32).rearrange("p (a b) -> p a b", a=4)
        otmp = carve(256).bitcast(F32)
        rc = carve(16).bitcast(F32)
        ssq = carve(16).bitcast(F32)
        att_end = off[0]
        ATT_KEYS = ["QTa", "QTb", "KT", "V1d", "V1f", "strip0", "strip1", "Wp0", "Wp1",
                    "E0", "E1", "E2", "E3", "E4", "E5", "P0", "P1", "P2", "P3", "P4", "P5", "mixstage", "o1", "oo", "osq",
                    "otmp", "rc", "ssq"]

        off[0] = 0
        xt4s = [carve(8192).bitcast(F32).rearrange("p (b n) -> p b n", b=4) for _ in range(2)]
        h2b = [carve(1024) for _ in range(4)]
        mt = carve(4096).rearrange("p (b n) -> p b n", b=4)
        mixT = carve(4096).rearrange("p (c t) -> p c t", c=8)
        h2T = carve(4096).rearrange("p (c t) -> p c t", c=8)
        aT = carve(NFC * 512).rearrange("p (c t) -> p c t", c=NFC)
        Wgu = [carve(2048).rearrange("p (c n) -> p c n", c=8) for _ in range(4)]
        sgb = [carve(1024).bitcast(F32) for _ in range(2)]
        FFN_KEYS = ["xt4_0", "xt4_1", "h2b0", "h2b1", "h2b2", "h2b3", "mt", "mixT", "h2T", "aT", "Wgu0", "Wgu1", "Wgu2", "Wgu3", "sg0", "sg1"]
        off[0] = 0
        xb4 = [carve(2048).bitcast(F32) for _ in range(4)]
        hb4 = [carve(1024) for _ in range(4)]
        A0_KEYS = ["xb4_%d" % i for i in range(4)] + ["hb4_%d" % i for i in range(4)]
        R2_KEYS = ATT_KEYS + FFN_KEYS + A0_KEYS + ["pro2"]

        def mm(out, lhsT, rhs, start, stop, reads, writes, skip=False):
            if skip:
                P.add("pe", lambda e: e.matmul(out, lhsT=lhsT, rhs=rhs, start=start, stop=stop,
                                               skip_group_check=True),
                      reads=reads, writes=writes, partial=True)
            else:
                P.add("pe", lambda e: e.matmul(out, lhsT=lhsT, rhs=rhs, start=start, stop=stop),
                      reads=reads, writes=writes, partial=True)

        def dma(eng, out_ap, in_ap, reads, writes, dsem, partial=False):
            return P.add(eng, lambda e: e.dma_start(out=out_ap, in_=in_ap),
                         reads=reads, writes=writes, partial=partial, dsem=dsem)

        def bcast_rows(t2d, row, ncols, nparts=128):
            return bass.AP(t2d.tensor, t2d.offset + row * t2d.shape[-1], [[0, nparts], [1, ncols]])

        def conv_ops(l):
            ops_ = []
            key = ("wb", l)
            sem = "wb%d" % l
            win = w_in[l].rearrange("(kc p) n -> p kc n", p=128)
            for p in range(8):
                for which in range(3):
                    if p < 4:
                        cb = which * 512 + 128 * p
                    else:
                        cb = 1536 + which * 512 + 128 * (p - 4)
                    src = win[:, :, cb:cb + 128]
                    dst = wb_in[l, p].rearrange("p (kc w j) -> p kc w j", kc=8, w=3)[:, :, which, :]
                    ops_.append(lambda s=src, d=dst: dma("pool", d, s, [], [key], sem, partial=True))
            src = w_out[l].rearrange("(kc p) n -> p kc n", p=128)
            dst = wb_out[l].rearrange("p (kc n) -> p kc n", kc=8)
            ops_.append(lambda s=src, d=dst: dma("pool", d, s, [], [key], sem, partial=True))
            wgu = w_gu[l].rearrange("(kc p) (u c j) -> p kc u c j", p=128, u=2, j=128)
            for c in range(NFC):
                for u in range(2):
                    src = wgu[:, :, u, c, :]
                    dst = wb_gu[l, c].rearrange("p (kc u j) -> p kc u j", kc=8, u=2)[:, :, u, :]
                    ops_.append(lambda s=src, d=dst: dma("pool", d, s, [], [key], sem, partial=True))
            src = w_down[l].rearrange("(c p) n -> p c n", p=128)
            dst = wb_down[l].rearrange("p (c n) -> p c n", c=NFC)
            ops_.append(lambda s=src, d=dst: dma("pool", d, s, [], [key], sem, partial=True))
            return ops_

        P.add("pool", lambda e: e.memset(identf[:], 1.0), writes=["identf"])
        P.add("pool", lambda e: e.affine_select(out=identf[:], in_=identf[:], pattern=[[-1, 128]],
                                                compare_op=ALU.is_equal, fill=0.0, base=0,
                                                channel_multiplier=1),
              reads=["identf"], writes=["identf"])
        P.add("dve", lambda e: e.tensor_copy(out=ident[:], in_=identf[:]), reads=["identf"], writes=["ident"])
        P.add("pool", lambda e: e.memset(Jf[:], 1.0), writes=["Jf"])
        P.add("pool", lambda e: e.affine_select(out=Jf[:], in_=Jf[:], pattern=[[1, 128]],
                                                compare_op=ALU.is_equal, fill=0.0, base=-127,
                                                channel_multiplier=1),
              reads=["Jf"], writes=["Jf"])
        P.add("pool", lambda e: e.memset(zeros[:], 0.0), writes=["zeros"])
        P.add("pool", lambda e: e.memset(mhalf[:], -0.5), writes=["mhalf"])
        P.add("pool", lambda e: e.memset(mhalf16[:], -0.5), writes=["mhalf16"])

        for op_ in conv_ops(0):
            op_()
        pending_conv = []

        def drip_conv(n):
            for _ in range(min(n, len(pending_conv))):
                pending_conv.pop(0)()

        if stop_after != 'P0':
            ohd = R1[0:32, 0:2 * LA_DIL].bitcast(F32)
            ohf = R1[0:32, 2 * LA_DIL:2 * LA_DIL + 2 * LA_DIFF].bitcast(F32)
            arow = R2[0:8, 0:2 * LA_DIFF].bitcast(F32)
            dma("sp", relb[:], rel_bias, [], ["relb"], "relb")
            dma("sp", ohd, oh_dil, [], ["pro1"], "pro1", partial=True)
            dma("sp", ohf, oh_diff, [], ["pro1"], "pro1", partial=True)
            P.add("act", lambda e: e.activation(out=expb[:], in_=relb[:], func=AF.Exp),
                  reads=["relb"], writes=["expb"])
            for (nh, h0, LA, ohv, adst) in ((8, 0, LA_DIL, ohd, a_dil), (4, 8, LA_DIFF, ohf, a_diff)):
                nch = (LA + 511) // 512
                for ci in range(nch):
                    c0 = ci * 512
                    w_ = min(512, LA - c0)
                    bk = ci % 2
                    mm(banks[bk][0:nh, 0:w_], expb[:, h0:h0 + nh], ohv[:, c0:c0 + w_], True, True,
                       ["expb", "pro1"], ["B%d" % bk])
                    P.add("dve", lambda e, bk=bk, c0=c0, w_=w_, nh=nh: e.tensor_copy(
                        out=arow[0:nh, c0:c0 + w_], in_=banks[bk][0:nh, 0:w_]),
                        reads=["B%d" % bk], writes=["pro2"], partial=True)
                dma("sp", adst, arow[0:nh, 0:LA], ["pro2"], [("a", h0)], "pro2")
            PROX = ["pro1_0", "pro1_1", "pro2_0", "pro2_1"]
            P.fence(["pro1", "pro2"] + PROX)
            Xvs = [R1[:, i * 2 * L_DIFF:(i + 1) * 2 * L_DIFF].bitcast(F32) for i in range(2)]
            Yvs = [R2[:, i * L_DIFF:(i + 1) * L_DIFF] for i in range(2)]
            for h in range(12):
                Xv = Xvs[h % 2]
                Yv = Yvs[h % 2]
                k1 = "pro1_%d" % (h % 2)
                k2 = "pro2_%d" % (h % 2)
                if h < 8:
                    L, asrc, sdst = L_DIL, a_dil, st_dil[h]
                    hrow = h
                else:
                    L, asrc, sdst = L_DIFF, a_diff, st_diff[h - 8]
                    hrow = h - 8
                src = bass.AP(asrc.tensor, asrc.offset + hrow * asrc.shape[-1], [[1, 128], [1, L]])
                dma("sp", Xv[:, 0:L], src, [("a", 0 if h < 8 else 8)], [k1], k1)
                for ci in range(L // 512):
                    bk = ci % 2
                    mm(banks[bk][:, :], Jf[:, :], Xv[:, ci * 512:(ci + 1) * 512], True, True,
                       ["Jf", k1], ["B%d" % bk])
                    eng = "act" if ci % 2 else "dve"
                    if eng == "act":
                        P.add("act", lambda e, bk=bk, ci=ci, Yv=Yv: e.copy(out=Yv[:, ci * 512:(ci + 1) * 512], in_=banks[bk][:, :]),
                              reads=["B%d" % bk], writes=[k2], partial=True)
                    else:
                        P.add("dve", lambda e, bk=bk, ci=ci, Yv=Yv: e.tensor_copy(out=Yv[:, ci * 512:(ci + 1) * 512], in_=banks[bk][:, :]),
                              reads=["B%d" % bk], writes=[k2], partial=True)
                dma("sp", sdst, Yv[:, 0:L], [k2], ["strips"], k2, partial=True)
            P.fence(R1_KEYS + R2_KEYS + PROX)

            if "strips" in debug:
                t = dbg_tensor("st_dil", [8, 128, L_DIL], BF16)
                finals.append(dma("sp", t, st_dil, ["strips"], ["dbg"], "dbg", partial=True))
                t = dbg_tensor("st_diff", [4, 128, L_DIFF], BF16)
                finals.append(dma("sp", t, st_diff, ["strips"], ["dbg"], "dbg", partial=True))


        def rms_to_bf16(x_ap, xkey, g_ap, gkey, i, out_ap, outkey):
            P.add("act", lambda e: e.activation(out=junk[:], in_=x_ap, func=AF.Square, accum_out=ssb[i][:]),
                  reads=[xkey], writes=["junk", "ss%d" % i])
            P.add("dve", lambda e: e.tensor_scalar(out=rsb[i][:], in0=ssb[i][:], scalar1=1.0 / D, scalar2=EPS,
                                                   op0=ALU.mult, op1=ALU.add),
                  reads=["ss%d" % i], writes=["rs%d" % i])
            P.add("pool", lambda e: e.tensor_tensor(out=rsb[i][:], in0=rsb[i][:], in1=mhalf[:, 0:1], op=ALU.pow),
                  reads=["rs%d" % i, "mhalf"], writes=["rs%d" % i])
            P.add("dve", lambda e: e.scalar_tensor_tensor(out=out_ap, in0=x_ap, scalar=rsb[i][:, 0:1], in1=g_ap,
                                                          op0=ALU.mult, op1=ALU.mult),
                  reads=[xkey, "rs%d" % i, gkey], writes=[outkey])

        tr_ctr = [0]

        def transpose_to(src_bf, srckey, dst3, dstkey, bank_list=(6, 7), evac="act"):
            bk = bank_list[tr_ctr[0] % len(bank_list)]
            tr_ctr[0] += 1
            bv = bankbf(bk)
            for c in range(8):
                P.add("pe", lambda e, c=c: e.transpose(bv[:, c * 128:(c + 1) * 128], src_bf[:, c * 128:(c + 1) * 128], ident[:]),
                      reads=[srckey, "ident"], writes=["B%d" % bk], partial=True)
            if evac == "act":
                P.add("act", lambda e: e.copy(out=dst3, in_=bv.rearrange("p (c t) -> p c t", c=8)),
                      reads=["B%d" % bk], writes=[dstkey], partial=True)
            else:
                P.add("dve", lambda e: e.tensor_copy(out=dst3, in_=bv.rearrange("p (c t) -> p c t", c=8)),
                      reads=["B%d" % bk], writes=[dstkey], partial=True)

        dma("sp", gFin[:], bcast_rows(g_final, 0, D), [], ["gFin"], "gFin")

        for l in range(n_layers if stop_after not in ('P0', 'P') else 0):
            lam_init = 0.8 - 0.6 * math.exp(-0.3 * l)
            dma("sp", gA[:], bcast_rows(g_attn, l, D), [], ["gA"], "gA")
            dma("sp", gF[:], bcast_rows(g_ffn, l, D), [], ["gF"], "gF")
            dma("sp", sgain[:], bcast_rows(subln_g, l, 128), [], ["sgain"], "sgain")
            lq_src = bass.AP(lambda_qk.tensor, lambda_qk.offset + l * 256, [[0, 128], [1, 256]])
            dma("sp", lq[:], lq_src, [], ["lq"], "lq")
            P.add("dve", lambda e, lam_init=lam_init: e.tensor_scalar(out=sgain[:], in0=sgain[:], scalar1=1.0 - lam_init, scalar2=None,
                                                   op0=ALU.mult),
                  reads=["sgain"], writes=["sgain"])
            lq4 = lq[:].rearrange("p (a b) -> p a b", a=4)
            lqp2 = lqp[:].rearrange("p (a b) -> p a b", a=2)
            P.add("dve", lambda e: e.tensor_tensor(out=lqp2[:, 0, :], in0=lq4[:, 0, :], in1=lq4[:, 1, :], op=ALU.mult),
                  reads=["lq"], writes=["lqp"], partial=True)
            P.add("dve", lambda e: e.tensor_tensor(out=lqp2[:, 1, :], in0=lq4[:, 2, :], in1=lq4[:, 3, :], op=ALU.mult),
                  reads=["lq"], writes=["lqp"], partial=True)
            P.add("dve", lambda e: e.reduce_sum(out=lsm[:, 0:2], in_=lqp2, axis=AX.X),
                  reads=["lqp"], writes=["lsm"])
            P.add("act", lambda e: e.activation(out=lsm[:, 2:4], in_=lsm[:, 0:2], func=AF.Exp),
                  reads=["lsm"], writes=["lsm2"])
            P.add("dve", lambda e: e.tensor_tensor(out=lsm[:, 4:5], in0=lsm[:, 3:4], in1=lsm[:, 2:3], op=ALU.subtract),
                  reads=["lsm2"], writes=["lsm3"])
            P.add("dve", lambda e, lam_init=lam_init: e.tensor_scalar(out=neglam[:], in0=lsm[:, 4:5], scalar1=-lam_init, scalar2=None,
                                                   op0=ALU.add),
                  reads=["lsm3"], writes=["neglam"])
            if l + 1 < n_layers:
                pending_conv.extend(conv_ops(l + 1))

            for s in range(n_seq):
                xsrc = x_in[s] if l == 0 else xs[s]
                last = (l == n_layers - 1)
                xdst = out[s] if last else xs[s]

                def pair_loads(p_):
                    wp_ = Wp[p_ % 2]
                    wkey_ = "Wp%d" % (p_ % 2)
                    dma("sp", wp_, wb_in[l, p_].rearrange("p (c n) -> p c n", c=8), [("wb", l)], [wkey_], wkey_)
                    stp_ = strips[p_ % 2]
                    skey_ = "strip%d" % (p_ % 2)
                    if p_ < 4:
                        dma("sp", stp_[:, 0:L_DIL], st_dil[2 * p_], ["strips"], [skey_], skey_, partial=True)
                        dma("sp", stp_[:, L_DIL:2 * L_DIL], st_dil[2 * p_ + 1], ["strips"], [skey_], skey_, partial=True)
                    else:
                        dma("sp", stp_[:, 0:L_DIFF], st_diff[p_ - 4], ["strips"], [skey_], skey_)

                pair_loads(0)
                for blk in range(NBLK + 2):
                    if blk < NBLK:
                        i = blk % 4
                        dma("sp", xb4[i][:, :], xsrc[blk * 128:(blk + 1) * 128, :], [("x", s, blk // 4)],
                            ["xb4_%d" % i], "xb4_%d" % i)
                        rms_to_bf16(xb4[i][:, :], "xb4_%d" % i, gA[:], "gA", i, hb4[i][:, :], "hb4_%d" % i)
                    if blk >= 2:
                        b2 = blk - 2
                        i2 = b2 % 4
                        transpose_to(hb4[i2], "hb4_%d" % i2, hT[:, :, b2 * 128:(b2 + 1) * 128], "hT",
                                     bank_list=(4, 5, 6, 7), evac=("act" if b2 % 2 == 0 else "dve"))
                P.fence(R2_KEYS)
                if stop_after == "A0":
                    break

                P.add("pool", lambda e: e.memset(V1d[:, :, 64:65], 1.0), writes=["V1d"], partial=True)
                P.add("pool", lambda e: e.memset(V1d[:, :, 129:130], 1.0), writes=["V1d"], partial=True)
                P.add("pool", lambda e: e.memset(V1f[:, :, 128:129], 1.0), writes=["V1f"], partial=True)
                P.add("pool", lambda e: e.memset(QTa[64:128, :], 0.0), writes=["QTa"], partial=True)
                P.add("pool", lambda e: e.memset(QTb[0:64, :], 0.0), writes=["QTb"], partial=True)
                tile_ctr = [0]
                ob_ctr = [0]
                for p in range(8):
                    is_dil = p < 4
                    wp = Wp[p % 2]
                    wkey = "Wp%d" % (p % 2)
                    stp = strips[p % 2]
                    skey = "strip%d" % (p % 2)
                    if p + 1 < 8:
                        pair_loads(p + 1)
                    pj = 0
                    for tt in range(8):
                        for which in (0, 1):
                            bk = (7, 0, 1, 2)[pj % 4]
                            pj += 1
                            for kc in range(8):
                                mm(banks[bk][:, :], wp[:, kc, which * 128:(which + 1) * 128],
                                   hT[:, kc, tt * 512:(tt + 1) * 512], kc == 0, kc == 7,
                                   [wkey, "hT"], ["B%d" % bk])
                            if which == 0:
                                P.add("act", lambda e, bk=bk, tt=tt: e.activation(
                                    out=QTa[0:64, tt * 512:(tt + 1) * 512], in_=banks[bk][0:64, :], func=AF.Copy, scale=0.125),
                                    reads=["B%d" % bk], writes=["QTa"], partial=True)
                                P.add("dve", lambda e, bk=bk, tt=tt: e.tensor_scalar(
                                    out=QTb[64:128, tt * 512:(tt + 1) * 512], in0=banks[bk][64:128, :], scalar1=0.125,
                                    scalar2=None, op0=ALU.mult),
                                    reads=["B%d" % bk], writes=["QTb"], partial=True)
                            else:
                                P.add("dve", lambda e, bk=bk, tt=tt: e.tensor_copy(
                                    out=KT[:, tt * 512:(tt + 1) * 512], in_=banks[bk][:, :]),
                                    reads=["B%d" % bk], writes=["KT"], partial=True)
                    for g in range(8):
                        bk = (7, 0, 1, 2)[pj % 4]
                        pj += 1
                        for b4 in range(4):
                            blk = 4 * g + b4
                            for kc in range(8):
                                mm(banks[bk][:, b4 * 128:(b4 + 1) * 128], hT[:, kc, blk * 128:(blk + 1) * 128],
                                   wp[:, kc, 256:384], kc == 0, kc == 7, [wkey, "hT"], ["B%d" % bk])
                        if is_dil:
                            dstv = V1d[:, 4 * g:4 * g + 4, :].rearrange("p b (h c) -> p b h c", h=2)[:, :, :, 0:64]
                            srcv = banks[bk][:, :].rearrange("p (b h c) -> p b h c", b=4, h=2)
                            eng = "act" if g % 2 else "dve"
                            if eng == "act":
                                P.add("act", lambda e, d=dstv, s_=srcv: e.copy(out=d, in_=s_),
                                      reads=["B%d" % bk], writes=["V1d"], partial=True)
                            else:
                                P.add("dve", lambda e, d=dstv, s_=srcv: e.tensor_copy(out=d, in_=s_),
                                      reads=["B%d" % bk], writes=["V1d"], partial=True)
                        else:
                            dstv = V1f[:, 4 * g:4 * g + 4, 0:128]
                            srcv = banks[bk][:, :].rearrange("p (b c) -> p b c", b=4)
                            eng = "act" if g % 2 else "dve"
                            if eng == "act":
                                P.add("act", lambda e, d=dstv, s_=srcv: e.copy(out=d, in_=s_),
                                      reads=["B%d" % bk], writes=["V1f"], partial=True)
                            else:
                                P.add("dve", lambda e, d=dstv, s_=srcv: e.tensor_copy(out=d, in_=s_),
                                      reads=["B%d" % bk], writes=["V1f"], partial=True)
                    if "qkv" in debug and l == 0 and s == 0 and p in (0, 4):
                        t = dbg_tensor("QT%d" % p, [128, 4096], BF16)
                        finals.append(dma("sp", t, QTa, ["QTa"], ["dbg"], "dbgq", partial=True))
                        t = dbg_tensor("KT%d" % p, [128, 4096], BF16)
                        finals.append(dma("sp", t, KT, ["KT"], ["dbg"], "dbgq", partial=True))
                        t = dbg_tensor("V%d" % p, [128, 32, 130], BF16)
                        vv = V1d if is_dil else V1f
                        finals.append(dma("sp", t, vv, ["V1d" if is_dil else "V1f"], ["dbg"], "dbgq", partial=True))

                    if stop_after == "A1":
                        break
                    if p == 7:
                        P.fence(R1_KEYS)
                        dma("sp", Wd, wb_down[l].rearrange("p (c n) -> p c n", c=NFC), [("wb", l)], ["Wd"], "Wd")
                        dma("sp", Wout, wb_out[l].rearrange("p (c n) -> p c n", c=8), [("wb", l)], ["Wout"], "Wout")
                    dv = 64 if is_dil else 128
                    accw = dv + 1
                    vkey = "V1d" if is_dil else "V1f"
                    V1 = V1d if is_dil else V1f
                    pend = []

                    def make_pv(j, m0, m1, pi, Oa, Ob, vcol0, is_last):
                        def f():
                            for qb in range(m0, m1):
                                bk = Oa if (is_dil or qb < 2) else Ob
                                c0 = (qb if is_dil else qb % 2) * accw
                                mm(banks[bk][:, c0:c0 + accw], Pbuf[pi][:, qb * 128:(qb + 1) * 128],
                                   V1[:, j, vcol0:vcol0 + accw], False, is_last,
                                   ["P%d" % pi, vkey], ["B%d" % bk], skip=True)
                        return f

                    def make_norm(t, v, Oa, Ob):
                        def Uview(qb):
                            bk = Oa if (is_dil or qb < 2) else Ob
                            c0 = (qb if is_dil else qb % 2) * accw
                            return banks[bk][:, c0:c0 + dv], banks[bk][:, c0 + dv:c0 + dv + 1], bk

                        def f():
                            for qb in range(4):
                                U, sden, bk = Uview(qb)
                                P.add("dve", lambda e, sden=sden, qb=qb: e.reciprocal(out=rc[:, qb:qb + 1], in_=sden),
                                      reads=["B%d" % bk], writes=["rc"], partial=True)
                            if is_dil:
                                for qb in range(4):
                                    U, sden, bk = Uview(qb)
                                    P.add("dve", lambda e, U=U, qb=qb: e.tensor_scalar(
                                        out=mixstage[:, 4 * t + qb, 64 * v:64 * v + 64], in0=U,
                                        scalar1=rc[:, qb:qb + 1], scalar2=None, op0=ALU.mult),
                                        reads=["B%d" % bk, "rc"], writes=["mixstage"], partial=True)
                            elif v == 0:
                                for qb in range(4):
                                    U, sden, bk = Uview(qb)
                                    P.add("dve", lambda e, U=U, qb=qb: e.tensor_scalar(
                                        out=o1[:, qb, :], in0=U, scalar1=rc[:, qb:qb + 1], scalar2=None, op0=ALU.mult),
                                        reads=["B%d" % bk, "rc"], writes=["o1"], partial=True)
                            else:
                                P.add("dve", lambda e: e.tensor_scalar(out=ssq[:, 0:4], in0=rc[:, 0:4], scalar1=neglam[:, 0:1],
                                                                       scalar2=None, op0=ALU.mult),
                                      reads=["rc", "neglam"], writes=["ssq"])
                                for qb in range(4):
                                    U, sden, bk = Uview(qb)
                                    P.add("dve", lambda e, U=U, qb=qb: e.scalar_tensor_tensor(
                                        out=mixstage[:, 4 * t + qb, :], in0=U, scalar=ssq[:, qb:qb + 1], in1=o1[:, qb, :],
                                        op0=ALU.mult, op1=ALU.add),
                                        reads=["B%d" % bk, "ssq", "o1"], writes=["mixstage"], partial=True)
                        return f

                    import os as _os
                    _tmax = int(_os.environ.get("K_TMAX", "8"))
                    _vset = [int(c) for c in _os.environ.get("K_VSET", "01")]
                    for t in range(_tmax):
                        for v in _vset:
                            pb = 64 * v
                            if is_dil:
                                strip = stp[:, v * L_DIL:(v + 1) * L_DIL]
                                vcol0 = 65 * v
                                jlo = max(0, 4 * t - 16)
                            else:
                                strip = stp[:, 0:L_DIFF]
                                vcol0 = 0
                                jlo = 0
                            oset = ob_ctr[0] % 2
                            ob_ctr[0] += 1
                            if is_dil:
                                Oa, Ob = 3 + oset, 3 + oset
                                zb = (Oa,)
                            else:
                                Oa, Ob = 3 + 2 * oset, 4 + 2 * oset
                                zb = (Oa, Ob)
                            zw = (4 if is_dil else 2) * accw
                            for bk in zb:
                                mm(banks[bk][:, 0:zw], zeros[:, 0:128], zeros[:, 0:zw], True, False,
                                   ["zeros"], ["B%d" % bk], skip=True)
                            jl = list(range(jlo, 4 * t + 4))
                            for j in jl:
                                m0 = max(0, j - 4 * t)
                                c0 = 128 * m0
                                D0 = 512 * t - 128 * j
                                c1 = min(512, 2176 - D0) if is_dil else 512
                                m1 = c1 // 128
                                ti = tile_ctr[0]
                                tile_ctr[0] += 1
                                if is_dil:
                                    sbk = (0, 1, 2, 7, 5, 6)[ti % 6]
                                    pi = ti % 6
                                else:
                                    sbk = (0, 1, 2, 7)[ti % 4]
                                    pi = ti % 4
                                QTv = QTb if v else QTa
                                mm(banks[sbk][:, c0:c1], KT[:, j * 128:(j + 1) * 128],
                                   QTv[:, 512 * t + c0:512 * t + c1], True, True,
                                   ["KT", "QTb" if v else "QTa"], ["B%d" % sbk])
                                P.add("act", lambda e, sbk=sbk, pi=pi, c0=c0, c1=c1: e.activation(
                                    out=Ebuf[pi][:, c0:c1], in_=banks[sbk][:, c0:c1], func=AF.Exp),
                                    reads=["B%d" % sbk], writes=["E%d" % pi])
                                P.add("dve", lambda e, pi=pi, c0=c0, c1=c1, D0=D0, strip=strip: e.tensor_tensor(
                                    out=Pbuf[pi][:, c0:c1], in0=Ebuf[pi][:, c0:c1],
                                    in1=strip[:, D0 + c0:D0 + c1], op=ALU.mult),
                                    reads=["E%d" % pi, skey], writes=["P%d" % pi])
                                pend.append(make_pv(j, m0, m1, pi, Oa, Ob, vcol0, j == jl[-1]))
                                if j == jl[-1]:
                                    pend.append(make_norm(t, v, Oa, Ob))
                                while len(pend) > (5 if is_dil else 3):
                                    pend.pop(0)()
                    while pend:
                        pend.pop(0)()
                    mdst = mixscr[s].rearrange("(b p) c -> p b c", p=128)[:, :, p * 128:(p + 1) * 128]
                    dma("sp", mdst, mixstage, ["mixstage"], [("mix", s, tt_) for tt_ in range(8)], "mixstage",
                        partial=True)
                    drip_conv(9)
                    if stop_after == "B1" and p == 0:
                        break
                if stop_after in ("B", "B1", "A1"):
                    break

                P.fence(R2_KEYS)
                pcnt = [0]

                def pbank():
                    b_ = pcnt[0] % 4
                    pcnt[0] += 1
                    return b_

                def c_loads(tt):
                    r0 = tt * 512
                    xk = "xt4_%d" % (tt % 2)
                    dma("pool", mt, mixscr[s, r0:r0 + 512, :].rearrange("(b p) c -> p b c", p=128),
                        [("mix", s, tt)], ["mt"], "mt")
                    dma("pool", xt4s[tt % 2], xsrc[r0:r0 + 512, :].rearrange("(b p) c -> p b c", p=128),
                        [("x", s, tt)], [xk], xk)

                def c_subln(tt):
                    junkf = junk[:, :].bitcast(F32)
                    for b in range(4):
                        mtd = mt[:, b, 512:1024]
                        P.add("dve", lambda e, mtd=mtd: e.tensor_tensor(out=junkf, in0=mtd, in1=mtd, op=ALU.mult),
                              reads=["mt"], writes=["junk"])
                        P.add("dve", lambda e, b=b: e.reduce_sum(out=sl4[:, 4 * b:4 * b + 4],
                                                                 in_=junkf.rearrange("p (h c) -> p h c", h=4), axis=AX.X),
                              reads=["junk"], writes=["sl4"], partial=True)
                    P.add("dve", lambda e: e.tensor_scalar(out=sl4[:, 0:16], in0=sl4[:, 0:16], scalar1=1.0 / 128,
                                                           scalar2=SUBLN_EPS, op0=ALU.mult, op1=ALU.add),
                          reads=["sl4"], writes=["sl4"])
                    P.add("pool", lambda e: e.tensor_tensor(out=sl4[:, 0:16], in0=sl4[:, 0:16], in1=mhalf16[:, 0:16], op=ALU.pow),
                          reads=["sl4", "mhalf16"], writes=["sl4"])
                    for b in range(4):
                        for h in range(4):
                            mv = mt[:, b, 512 + 128 * h:512 + 128 * (h + 1)]
                            P.add("dve", lambda e, mv=mv, b=b, h=h: e.scalar_tensor_tensor(
                                out=mv, in0=mv, scalar=sl4[:, 4 * b + h:4 * b + h + 1], in1=sgain[:],
                                op0=ALU.mult, op1=ALU.mult),
                                reads=["mt", "sl4", "sgain"], writes=["mt"], partial=True)

                def c_pe1(tt):
                    xt4 = xt4s[tt % 2]
                    xk = "xt4_%d" % (tt % 2)
                    for b in range(4):
                        transpose_to(mt[:, b, :], "mt", mixT[:, :, b * 128:(b + 1) * 128], "mixT",
                                     bank_list=(4, 5, 6, 7), evac=("act" if b % 2 == 0 else "dve"))
                    for b in range(4):
                        for half in range(2):
                            bk = pbank()
                            for kc in range(8):
                                mm(banks[bk][:, :], mixT[:, kc, b * 128:(b + 1) * 128],
                                   Wout[:, kc, half * 512:(half + 1) * 512], kc == 0, kc == 7,
                                   ["mixT", "Wout"], ["B%d" % bk])
                            P.add("dve", lambda e, bk=bk, b=b, half=half, xt4=xt4: e.tensor_tensor(
                                out=xt4[:, b, half * 512:(half + 1) * 512], in0=banks[bk][:, :],
                                in1=xt4[:, b, half * 512:(half + 1) * 512], op=ALU.add),
                                reads=["B%d" % bk, xk], writes=[xk], partial=True)

                def c_norm_elem(tt):
                    xt4 = xt4s[tt % 2]
                    xk = "xt4_%d" % (tt % 2)
                    for b in range(4):
                        rms_to_bf16(xt4[:, b, :], xk, gF[:], "gF", b, h2b[b][:, :], "h2b%d" % b)

                def c_norm_T(tt):
                    for b in range(4):
                        transpose_to(h2b[b], "h2b%d" % b, h2T[:, :, b * 128:(b + 1) * 128], "h2T",
                                     bank_list=(4, 5, 6, 7), evac=("act" if b % 2 == 0 else "dve"))

                def c_gu(tt):
                    for c in range(NFC):
                        wi = c % 4
                        wk = "Wgu%d" % wi
                        dma("sp", Wgu[wi], wb_gu[l, c].rearrange("p (c n) -> p c n", c=8), [("wb", l)], [wk], wk)
                        bg = pbank()
                        for kc in range(8):
                            mm(banks[bg][:, :], Wgu[wi][:, kc, 0:128], h2T[:, kc, :], kc == 0, kc == 7,
                               [wk, "h2T"], ["B%d" % bg])
                        bu = pbank()
                        for kc in range(8):
                            mm(banks[bu][:, :], Wgu[wi][:, kc, 128:256], h2T[:, kc, :], kc == 0, kc == 7,
                               [wk, "h2T"], ["B%d" % bu])
                        si = c % 2
                        P.add("act", lambda e, bg=bg, si=si: e.activation(out=sgb[si][:, :], in_=banks[bg][:, :], func=AF.Silu),
                              reads=["B%d" % bg], writes=["sg%d" % si])
                        P.add("dve", lambda e, bu=bu, si=si, c=c: e.tensor_tensor(
                            out=aT[:, c, :], in0=sgb[si][:, :], in1=banks[bu][:, :], op=ALU.mult),
                            reads=["sg%d" % si, "B%d" % bu], writes=["aT"], partial=True)

                def c_down(tt):
                    r0 = tt * 512
                    xt4 = xt4s[tt % 2]
                    xk = "xt4_%d" % (tt % 2)
                    for b in range(4):
                        for half in range(2):
                            bk = pbank()
                            for c in range(NFC):
                                mm(banks[bk][:, :], aT[:, c, b * 128:(b + 1) * 128],
                                   Wd[:, c, half * 512:(half + 1) * 512], c == 0, c == NFC - 1,
                                   ["aT", "Wd"], ["B%d" % bk])
                            P.add("dve", lambda e, bk=bk, b=b, half=half, xt4=xt4: e.tensor_tensor(
                                out=xt4[:, b, half * 512:(half + 1) * 512], in0=banks[bk][:, :],
                                in1=xt4[:, b, half * 512:(half + 1) * 512], op=ALU.add),
                                reads=["B%d" % bk, xk], writes=[xk], partial=True)
                    if last:
                        for b in range(4):
                            i = b
                            xb_ = xt4[:, b, :]
                            P.add("act", lambda e, xb_=xb_, i=i: e.activation(out=junk[:], in_=xb_, func=AF.Square,
                                                                             accum_out=ssb[i][:]),
                                  reads=[xk], writes=["junk", "ss%d" % i])
                            P.add("dve", lambda e, i=i: e.tensor_scalar(out=rsb[i][:], in0=ssb[i][:], scalar1=1.0 / D,
                                                                        scalar2=EPS, op0=ALU.mult, op1=ALU.add),
                                  reads=["ss%d" % i], writes=["rs%d" % i])
                            P.add("pool", lambda e, i=i: e.tensor_tensor(out=rsb[i][:], in0=rsb[i][:], in1=mhalf[:, 0:1],
                                                                         op=ALU.pow),
                                  reads=["rs%d" % i, "mhalf"], writes=["rs%d" % i])
                            P.add("dve", lambda e, xb_=xb_, i=i: e.scalar_tensor_tensor(
                                out=xb_, in0=xb_, scalar=rsb[i][:, 0:1], in1=gFin[:], op0=ALU.mult, op1=ALU.mult),
                                reads=[xk, "rs%d" % i, "gFin"], writes=[xk], partial=True)
                    st = dma("sp", xdst[r0:r0 + 512, :].rearrange("(b p) c -> p b c", p=128), xt4,
                             [xk], [("x", s, tt)], xk)
                    if last:
                        finals.append(st)

                c_loads(0)
                c_subln(0)
                c_pe1(0)
                c_norm_elem(0)
                c_norm_T(0)
                for tt in range(8):
                    if tt + 1 < 8:
                        c_loads(tt + 1)
                        c_subln(tt + 1)
                    c_gu(tt)
                    if tt + 1 < 8:
                        c_pe1(tt + 1)
                        c_norm_elem(tt + 1)
                    c_down(tt)
                    if tt + 1 < 8:
                        c_norm_T(tt + 1)
                    drip_conv(3)
                P.fence(R1_KEYS + R2_KEYS)
                drip_conv(1000)
            if stop_after is not None:
                break

        if "hT" in debug:
            t = dbg_tensor("hT", [128, 8, 4096], BF16)
            finals.append(dma("sp", t, hT, ["hT"], ["dbg"], "dbgh"))
        if "mix" in debug:
            t = dbg_tensor("mix", [n_seq, S, D], BF16)
            finals.append(dma("sp", t, mixscr, [("mix", s_, tt_) for s_ in range(n_seq) for tt_ in range(8)],
                              ["dbg"], "dbgm"))
        if "xs" in debug:
            t = dbg_tensor("xs", [n_seq, S, D], F32)
            finals.append(dma("sp", t, xs, [("x", s_, tt_) for s_ in range(n_seq) for tt_ in range(8)],
                              ["dbg"], "dbgx"))
        if not finals:
            finals.append(P.fence(R1_KEYS + R2_KEYS))
        P.emit(final_waits=finals)
    nc._dbg_out = dbg_out
    nc._prog = P
    return nc


_ARG_ORDER = ("g_attn", "w_in", "w_out", "rel_bias", "lambda_qk", "subln_g", "g_ffn",
              "w_gate_up", "w_down", "g_final")


def make_in_maps(inputs, n_cores=8, n_seq=2):
    ohd, ohf = _onehots()
    shared = {}
    for k in _ARG_ORDER:
        a = np.ascontiguousarray(np.asarray(inputs[k], dtype=np.float32))
        if k == "g_final":
            a = a.reshape(1, D)
        shared[k] = a
    shared["oh_dil"] = ohd
    shared["oh_diff"] = ohf
    x = np.asarray(inputs["x"], dtype=np.float32)
    maps = []
    for c in range(n_cores):
        m = dict(shared)
        m["x"] = np.ascontiguousarray(x[c * n_seq:(c + 1) * n_seq])
        maps.append(m)
    return maps


def kernel(x, g_attn, w_in, w_out, rel_bias, lambda_qk, subln_g, g_ffn, w_gate_up, w_down, g_final):
    inputs = dict(x=x, g_attn=g_attn, w_in=w_in, w_out=w_out, rel_bias=rel_bias, lambda_qk=lambda_qk,
                  subln_g=subln_g, g_ffn=g_ffn, w_gate_up=w_gate_up, w_down=w_down, g_final=g_final)
    nc = build_program()
    in_maps = make_in_maps(inputs)
    res = run_bass_kernel_spmd(nc, in_maps, core_ids=list(range(8)))
    outs = [np.asarray(r["out"]) for r in res.results]
    return np.concatenate(outs, axis=0).astype(np.float32)
```
